# Optimizing a Trainium2 kernel written in Bass

```python
import functools
import jax, jax.numpy as jnp
from jax import lax
import numpy as np

D_MODEL = 2048
BATCH = 16
SEQ = 2048
DEPTH = 2
DEC_BATCH = 16
DEC_SEQ = 32
PAST_LEN = 4096

CHUNK = 64
M_HEADS = 4
M_DK = 128
M_DV = 256
M_WIDTH = M_HEADS * M_DV
A_HEADS = 16
A_KV_HEADS = 4
A_HEAD_DIM = 64
A_GROUP = A_HEADS // A_KV_HEADS
A_WIDTH = A_HEADS * A_HEAD_DIM
WINDOW = 128
N_WIN_CHUNKS = WINDOW // CHUNK
N_BUCKETS = 32
MAX_DISTANCE = 128
D_FF = 4 * D_MODEL
EPS = 1e-6
NEG_INF = -1e30
SPLITS = (M_HEADS * M_DK, M_HEADS * M_DK, M_WIDTH, M_WIDTH, M_HEADS, M_HEADS,
          A_WIDTH, A_KV_HEADS * A_HEAD_DIM, A_KV_HEADS * A_HEAD_DIM)
D_IN = sum(SPLITS)

kernel_name = 'hymba_mlstm_swa_sink_stream_step'


def rms_norm(x, g):
    xf = x.astype(jnp.float32)
    y = xf * lax.rsqrt(jnp.mean(xf * xf, axis=-1, keepdims=True) + EPS)
    return (y * g.astype(jnp.float32)).astype(x.dtype)


def t5_bucket(rel):
    half = N_BUCKETS // 2
    exact = half // 2
    n = np.abs(rel)
    large = exact + (np.log(np.maximum(n, 1) / exact) / np.log(MAX_DISTANCE / exact) * (half - exact)).astype(np.int32)
    large = np.minimum(large, half - 1)
    return (rel > 0).astype(np.int32) * half + np.where(n < exact, n, large).astype(np.int32)


def band_bias(rel_bias, n_prev, lq, lk):
    rel = (np.arange(lk)[None, :] - n_prev) - np.arange(lq)[:, None]
    b = rel_bias[jnp.asarray(t5_bucket(rel))]
    return b.transpose(2, 0, 1).reshape(A_KV_HEADS, A_GROUP, lq, lk).astype(jnp.float32)


def sink_attention(q, k, v, bias, sink, valid=None):
    s = jnp.einsum('...qhgd,...khd->...hgqk', q, k).astype(jnp.float32) * A_HEAD_DIM ** -0.5 + bias
    if valid is not None:
        s = jnp.where(valid, s, NEG_INF)
    sk = sink.astype(jnp.float32).reshape(A_KV_HEADS, A_GROUP, 1)
    mx = jnp.maximum(s.max(-1), sk)
    p = jnp.exp(s - mx[..., None])
    p = p / (p.sum(-1) + jnp.exp(sk - mx))[..., None]
    return jnp.einsum('...hgqk,...khd->...qhgd', p.astype(v.dtype), v)


def attn_prompt(q, k, v, rel_bias, sink):
    B, S = q.shape[:2]
    nc = S // CHUNK
    lk = WINDOW + CHUNK
    qc = q.reshape(B, nc, CHUNK, A_KV_HEADS, A_GROUP, A_HEAD_DIM)

    def band(a):
        ap = jnp.pad(a, ((0, 0), (WINDOW, 0), (0, 0), (0, 0))).reshape(B, nc + N_WIN_CHUNKS, CHUNK, A_KV_HEADS, A_HEAD_DIM)
        return jnp.concatenate([ap[:, w:w + nc] for w in range(N_WIN_CHUNKS + 1)], axis=2)

    key_pos = np.arange(nc)[:, None] * CHUNK + np.arange(lk)[None, :] - WINDOW
    valid = jnp.asarray(key_pos >= 0)[:, None, None, None, :]
    o = sink_attention(qc, band(k), band(v), band_bias(rel_bias, WINDOW, CHUNK, lk), sink, valid)
    return o.reshape(B, S, A_WIDTH), (k[:, -WINDOW:], v[:, -WINDOW:])


def attn_sample(q, k, v, ck, cv, rel_bias, sink):
    B, L = q.shape[:2]
    n_win = ck.shape[1]
    kk = jnp.concatenate([ck.astype(k.dtype), k], axis=1)
    vv = jnp.concatenate([cv.astype(v.dtype), v], axis=1)
    o = sink_attention(q.reshape(B, L, A_KV_HEADS, A_GROUP, A_HEAD_DIM), kk, vv,
                       band_bias(rel_bias, n_win, L, n_win + L), sink)
    return o.reshape(B, L, A_WIDTH), (kk[:, -n_win:], vv[:, -n_win:])


def mlstm_chunk(carry, xs):
    C, n, m = carry
    q, k, v, ig, logf = xs
    L = q.shape[1]
    b = jnp.cumsum(logf, axis=1).transpose(0, 2, 1)
    igh = ig.transpose(0, 2, 1)
    causal = jnp.tril(jnp.ones((L, L), dtype=bool))
    D = jnp.where(causal, b[..., :, None] - b[..., None, :] + igh[..., None, :], NEG_INF)
    inter = b + m[..., None]
    m_t = jnp.maximum(inter, D.max(-1))
    w = jnp.exp(D - m_t[..., None]) * jnp.einsum('blhk,bshk->bhls', q, k)
    g = jnp.exp(inter - m_t)
    num = jnp.einsum('bhls,bshv->blhv', w, v) + jnp.einsum('bhl,bhvk,blhk->blhv', g, C, q)
    den = w.sum(-1) + g * jnp.einsum('bhk,blhk->bhl', n, q)
    h = num / jnp.maximum(jnp.abs(den), jnp.exp(-m_t)).transpose(0, 2, 1)[..., None]
    m_end = m_t[..., -1]
    wk = jnp.exp(b[..., -1:] - b + igh - m_end[..., None])
    decay = jnp.exp(inter[..., -1] - m_end)
    C_new = decay[..., None, None] * C + jnp.einsum('bhs,bshv,bshk->bhvk', wk, v, k)
    n_new = decay[..., None] * n + jnp.einsum('bhs,bshk->bhk', wk, k)
    return (C_new, n_new, m_end), h


def mlstm_prompt(q, k, v, ig, logf):
    B, S = q.shape[:2]
    nc = S // CHUNK

    def chunks(a):
        return a.reshape(B, nc, CHUNK, *a.shape[2:]).swapaxes(0, 1)

    init = (jnp.zeros((B, M_HEADS, M_DV, M_DK), jnp.float32),
            jnp.zeros((B, M_HEADS, M_DK), jnp.float32),
            jnp.zeros((B, M_HEADS), jnp.float32))
    state, h = lax.scan(mlstm_chunk, init, (chunks(q), chunks(k), chunks(v), chunks(ig), chunks(logf)))
    return h.swapaxes(0, 1).reshape(B, S, M_HEADS, M_DV), state


def mlstm_sample(q, k, v, ig, logf, C, n, m):
    f32 = jnp.float32
    state, h = mlstm_chunk((C.astype(f32), n.astype(f32), m.astype(f32)), (q, k, v, ig, logf))
    return h, state


def trunk_layer(x, attn_fn, mlstm_fn, g_mix, w_in, b_i, b_f, g_q, g_k, g_mo, g_ao, w_out, g_ffn, w_up, w_down):
    B, L, _ = x.shape
    f32 = jnp.float32
    h = rms_norm(x, g_mix)
    mq, mk, mv, mo, mi, mf, aq, ak, av = jnp.split(h @ w_in, np.cumsum(SPLITS)[:-1].tolist(), axis=-1)
    hm, m_state = mlstm_fn(mq.reshape(B, L, M_HEADS, M_DK).astype(f32),
                           mk.reshape(B, L, M_HEADS, M_DK).astype(f32) * M_DK ** -0.5,
                           mv.reshape(B, L, M_HEADS, M_DV).astype(f32),
                           (mi + b_i).astype(f32),
                           jax.nn.log_sigmoid((mf + b_f).astype(f32)))
    hm = rms_norm(hm, g_mo.reshape(M_HEADS, M_DV)).reshape(B, L, M_WIDTH).astype(x.dtype) * jax.nn.sigmoid(mo)
    ha, kv_state = attn_fn(rms_norm(aq.reshape(B, L, A_HEADS, A_HEAD_DIM), g_q),
                           rms_norm(ak.reshape(B, L, A_KV_HEADS, A_HEAD_DIM), g_k),
                           av.reshape(B, L, A_KV_HEADS, A_HEAD_DIM))
    ha = rms_norm(ha, g_ao)
    x = x + jnp.concatenate([hm, ha], axis=-1) @ w_out
    u = rms_norm(x, g_ffn) @ w_up
    x = x + jnp.square(jax.nn.relu(u)) @ w_down
    return x, kv_state, m_state


def setup_inputs(seed: int = 0) -> dict:
    key = jax.random.key(seed)
    ks = jax.random.split(key, 24)
    n_win = min(WINDOW, PAST_LEN)

    def nrm(k, shape, scale):
        return scale * jax.random.normal(k, shape, jnp.float32)

    def gain(k, shape):
        return 1.0 + 0.05 * jax.random.normal(k, shape, jnp.float32)

    return {
        'x_prompt': nrm(ks[0], (BATCH, SEQ, D_MODEL), 1.0),
        'x_sample': nrm(ks[1], (DEC_BATCH, DEC_SEQ, D_MODEL), 1.0),
        'cache_k': nrm(ks[2], (DEPTH, DEC_BATCH, n_win, A_KV_HEADS, A_HEAD_DIM), 1.0),
        'cache_v': nrm(ks[3], (DEPTH, DEC_BATCH, n_win, A_KV_HEADS, A_HEAD_DIM), 1.0),
        'state_C': nrm(ks[4], (DEPTH, DEC_BATCH, M_HEADS, M_DV, M_DK), 0.5),
        'state_n': nrm(ks[5], (DEPTH, DEC_BATCH, M_HEADS, M_DK), 0.5),
        'state_m': nrm(ks[6], (DEPTH, DEC_BATCH, M_HEADS), 1.0),
        'rel_bias': nrm(ks[7], (N_BUCKETS, A_HEADS), 0.5),
        'g_mix': gain(ks[8], (DEPTH, D_MODEL)),
        'w_in': nrm(ks[9], (DEPTH, D_MODEL, D_IN), D_MODEL ** -0.5),
        'b_i': nrm(ks[10], (DEPTH, M_HEADS), 0.1),
        'b_f': 3.0 + nrm(ks[11], (DEPTH, M_HEADS), 0.5),
        'g_q': gain(ks[12], (DEPTH, A_HEAD_DIM)),
        'g_k': gain(ks[13], (DEPTH, A_HEAD_DIM)),
        'sinks': nrm(ks[14], (DEPTH, A_HEADS), 0.5),
        'g_mo': gain(ks[15], (DEPTH, M_WIDTH)),
        'g_ao': gain(ks[16], (DEPTH, A_WIDTH)),
        'w_out': nrm(ks[17], (DEPTH, M_WIDTH + A_WIDTH, D_MODEL), (M_WIDTH + A_WIDTH) ** -0.5),
        'g_ffn': gain(ks[18], (DEPTH, D_MODEL)),
        'w_up': nrm(ks[19], (DEPTH, D_MODEL, D_FF), D_MODEL ** -0.5),
        'w_down': nrm(ks[20], (DEPTH, D_FF, D_MODEL), D_FF ** -0.5),
    }


def reference(x_prompt, x_sample, cache_k, cache_v, state_C, state_n, state_m, rel_bias,
              g_mix, w_in, b_i, b_f, g_q, g_k, sinks, g_mo, g_ao, w_out, g_ffn, w_up, w_down):
    xp, xs = x_prompt, x_sample
    p_k, p_v, p_C, p_n, p_m = [], [], [], [], []
    s_k, s_v, s_C, s_n, s_m = [], [], [], [], []
    for l in range(DEPTH):
        w = (g_mix[l], w_in[l], b_i[l], b_f[l], g_q[l], g_k[l], g_mo[l], g_ao[l], w_out[l], g_ffn[l], w_up[l], w_down[l])
        xp, (k_, v_), (C_, n_, m_) = trunk_layer(
            xp, functools.partial(attn_prompt, rel_bias=rel_bias, sink=sinks[l]), mlstm_prompt, *w)
        p_k.append(k_); p_v.append(v_); p_C.append(C_); p_n.append(n_); p_m.append(m_)
        xs, (k_, v_), (C_, n_, m_) = trunk_layer(
            xs,
            functools.partial(attn_sample, ck=cache_k[l], cv=cache_v[l], rel_bias=rel_bias, sink=sinks[l]),
            functools.partial(mlstm_sample, C=state_C[l], n=state_n[l], m=state_m[l]), *w)
        s_k.append(k_); s_v.append(v_); s_C.append(C_); s_n.append(n_); s_m.append(m_)
    return (xp, xs,
            jnp.stack(p_k), jnp.stack(p_v), jnp.stack(p_C), jnp.stack(p_n), jnp.stack(p_m),
            jnp.stack(s_k), jnp.stack(s_v), jnp.stack(s_C), jnp.stack(s_n), jnp.stack(s_m))
```

```python
import math
from contextlib import ExitStack

import numpy as np
import concourse.bass as bass
import concourse.mybir as mybir
from concourse.bass_utils import run_bass_kernel_spmd

F32 = mybir.dt.float32
BF16 = mybir.dt.bfloat16
AF = mybir.ActivationFunctionType
ALU = mybir.AluOpType
AX = mybir.AxisListType

ENGS = ("pe", "act", "dve", "pool", "sp")
EPS = 1e-6
DIN = 4616
NEG = -30000.0


class Prog:
    def __init__(self, nc, es, same_engine_sync=True):
        self.nc = nc
        self.es = es
        self.q = {e: [] for e in ENGS}
        self.sems = {}
        self.count = {}
        self.waited = {e: {} for e in ENGS}
        self.lastw = {}
        self.readers = {}
        self.same_engine_sync = same_engine_sync
        self.nops = 0
        self.nwaits = 0
        import os
        self.max_ops = int(os.environ.get("KMAX", "0"))
        self.marks = []
        self.pe_log = []
        self.phase = "const"

    def sem(self, key):
        if key not in self.sems:
            self.sems[key] = self.es.enter_context(self.nc.semaphore("s_%s" % (key,)))
            self.count[key] = 0
        return self.sems[key]

    def op(self, eng, fn, reads=(), writes=(), semkey=None, inc=1):
        own = "E_" + eng
        if semkey is None:
            semkey = own
        if self.max_ops and self.nops >= self.max_ops:
            return None
        self.sem(semkey)
        deps = {}

        def add(d, kind):
            k, v = d
            if k == own:
                if eng == "pe" or not self.same_engine_sync:
                    return
            if deps.get(k, 0) < v:
                deps[k] = v

        for r in reads:
            if r in self.lastw:
                add(self.lastw[r], "raw")
        for w in writes:
            if w in self.lastw:
                add(self.lastw[w], "waw")
            for d in self.readers.get(w, ()):
                add(d, "war")
        for k, v in deps.items():
            if self.waited[eng].get(k, 0) < v:
                self.waited[eng][k] = v
                self.q[eng].append(("wait", self.sems[k], v))
                self.nwaits += 1
        self.count[semkey] += inc
        val = self.count[semkey]
        self.q[eng].append(("op", fn, self.sems[semkey], inc))
        self.nops += 1
        for w in writes:
            self.lastw[w] = (semkey, val)
            self.readers[w] = []
        for r in reads:
            self.readers.setdefault(r, []).append((semkey, val))
        return (semkey, val)

    def wait_all(self, eng, pred):
        for k, v in self.count.items():
            if not pred(k):
                continue
            if v > 0 and self.waited[eng].get(k, 0) < v:
                self.waited[eng][k] = v
                self.q[eng].append(("wait", self.sems[k], v))

    def emit(self):
        nc = self.nc
        with nc.Block() as block:
            def run(eng_name):
                def body(e):
                    for it in self.q[eng_name]:
                        if it[0] == "wait":
                            e.wait_ge(it[1], it[2])
                        else:
                            ins = it[1](e)
                            ins.then_inc(it[2], it[3])
                return body
            block.tensor(run("pe"))
            block.scalar(run("act"))
            block.vector(run("dve"))
            block.gpsimd(run("pool"))
            block.sync(run("sp"))


def t5_bucket(rel):
    half = 16
    exact = 8
    n = np.abs(rel)
    large = exact + (np.log(np.maximum(n, 1) / exact) / np.log(128 / exact) * (half - exact)).astype(np.int32)
    large = np.minimum(large, half - 1)
    return (rel > 0).astype(np.int32) * half + np.where(n < exact, n, large).astype(np.int32)


def make_onehot():
    oh = np.zeros((2, 33, 128, 128), np.float32)
    i = np.arange(128)[:, None]
    p = np.arange(128)[None, :]
    for tbl in range(2):
        rel = (p - 128 - i) if tbl == 0 else (p - i)
        bk = t5_bucket(rel)
        for b in range(32):
            oh[tbl, b] = (bk == b)
        if tbl == 0:
            oh[tbl, 32] = (p < 64) & (i >= 64)
        else:
            oh[tbl, 32] = (p >= 64) & (i < 64)
    return oh


def full_cfg():
    return dict(D=2048, DFF=8192, SEQ=2048, NSEQ=2, NS=2, LS=32, NW=2)


def build_program(cfg):
    D = cfg["D"]; DFF = cfg["DFF"]; SEQ = cfg["SEQ"]; NSEQ = cfg["NSEQ"]; NS = cfg["NS"]; LS = cfg["LS"]
    NW = cfg.get("NW", 2)
    KC = D // 128
    T = 512
    DCW = min(512, D); NDC = D // DCW
    HH = min(4096, DFF); NHALF = DFF // HH; NHC = HH // 128
    UPS = HH // 512
    DNK = min(16, NHC); DNS = NHC // DNK
    NPASS_SEQ = SEQ // T
    SCL = 128 ** -0.5

    nc = bass.Bass("TRN2", target_bir_lowering=False)

    def din(name, shape):
        return nc.dram_tensor(name, list(shape), F32, kind="ExternalInput").ap()

    def dout(name, shape):
        return nc.dram_tensor(name, list(shape), F32, kind="ExternalOutput").ap()

    xp = din("xp", [NSEQ, SEQ, D]); xsm = din("xs", [NS, LS, D])
    ck = din("ck", [2, NS, 128, 256]); cv = din("cv", [2, NS, 128, 256])
    sC = din("sC", [2, NS, 4, 256, 128]); sn = din("sn", [2, NS, 4, 128]); sm = din("sm", [2, NS, 4])
    rel_bias = din("rel_bias", [32, 16])
    g_mix = din("g_mix", [2, D]); w_in = din("w_in", [2, D, DIN])
    b_i = din("b_i", [2, 4]); b_f = din("b_f", [2, 4])
    g_q = din("g_q", [2, 64]); g_k = din("g_k", [2, 64]); sinks = din("sinks", [2, 16])
    g_mo = din("g_mo", [2, 1024]); g_ao = din("g_ao", [2, 1024])
    w_out = din("w_out", [2, 2048, D]); g_ffn = din("g_ffn", [2, D])
    w_up = din("w_up", [2, D, DFF]); w_down = din("w_down", [2, DFF, D])
    oh = din("oh", [2, 33, 128, 128])

    y_p = dout("y_p", [NSEQ, SEQ, D]); y_s = dout("y_s", [NS, LS, D])
    p_k = dout("p_k", [2, NSEQ, 128, 256]); p_v = dout("p_v", [2, NSEQ, 128, 256])
    p_C = dout("p_C", [2, NSEQ, 4, 256, 128]); p_n = dout("p_n", [2, NSEQ, 4, 128]); p_m = dout("p_m", [2, NSEQ, 4])
    s_k = dout("s_k", [2, NS, 128, 256]); s_v = dout("s_v", [2, NS, 128, 256])
    s_C = dout("s_C", [2, NS, 4, 256, 128]); s_n = dout("s_n", [2, NS, 4, 128]); s_m = dout("s_m", [2, NS, 4])

    es = ExitStack()
    P = Prog(nc, es)

    def sb(name, shape, dt):
        return es.enter_context(nc.sbuf_tensor(name, list(shape), dt))

    def ACT(out, in_, func, reads, writes, bias=None, scale=None, accum=None):
        kw_ = {}
        if bias is not None:
            kw_["bias"] = bias
        if scale is not None:
            kw_["scale"] = scale
        if accum is not None:
            kw_["accum_out"] = accum
        P.op("act", lambda e: e.activation(out=out, in_=in_, func=func, **kw_), reads=reads, writes=writes)

    def TT(eng, out, in0, in1, op, reads, writes):
        P.op(eng, lambda e: e.tensor_tensor(out=out, in0=in0, in1=in1, op=op), reads=reads, writes=writes)

    def TS(eng, out, in0, s1, s2, op0, op1, reads, writes):
        if op1 is None:
            P.op(eng, lambda e: e.tensor_scalar(out=out, in0=in0, scalar1=s1, scalar2=None, op0=op0),
                 reads=reads, writes=writes)
        else:
            P.op(eng, lambda e: e.tensor_scalar(out=out, in0=in0, scalar1=s1, scalar2=s2, op0=op0, op1=op1),
                 reads=reads, writes=writes)

    def STT(out, in0, scalar, in1, op0, op1, reads, writes):
        P.op("dve", lambda e: e.scalar_tensor_tensor(out=out, in0=in0, scalar=scalar, in1=in1, op0=op0, op1=op1),
             reads=reads, writes=writes)

    def TC(eng, out, in_, reads, writes):
        if eng == "act":
            ACT(out, in_, AF.Copy, reads, writes)
        else:
            P.op(eng, lambda e: e.tensor_copy(out=out, in_=in_), reads=reads, writes=writes)

    def RECIP(out, in_, reads, writes):
        P.op("dve", lambda e: e.reciprocal(out=out, in_=in_), reads=reads, writes=writes)

    def RECIP_LP(out, in_, reads, writes):
        def fn(e):
            with nc.allow_low_precision(reason="gate value is stored bf16 (it only multiplies a bf16 matmul operand)"):
                return e.reciprocal(out=out, in_=in_)
        P.op("dve", fn, reads=reads, writes=writes)

    def SCAN(out, d0, d1, init, op0, op1, reads, writes):
        P.op("dve", lambda e: e.tensor_tensor_scan(out=out, data0=d0, data1=d1, initial=init, op0=op0, op1=op1),
             reads=reads, writes=writes)

    def REDUCE(out, in_, reads, writes):
        P.op("dve", lambda e: e.tensor_reduce(out=out, in_=in_, axis=AX.X, op=ALU.add), reads=reads, writes=writes)

    def MEMSET(eng, ap, val, writes):
        P.op(eng, lambda e: e.memset(ap, val), writes=writes)

    def ASEL(out, in_, pattern, cmp, base, cm, reads, writes):
        P.op("pool", lambda e: e.affine_select(out=out, in_=in_, pattern=pattern, compare_op=cmp, fill=0.0,
                                               base=base, channel_multiplier=cm), reads=reads, writes=writes)

    def DMA(q, out, in_, reads, writes, semkey, slow=False):
        if slow:
            P.op(q, lambda e: e.dma_start(out=out, in_=in_, allow_slow_non_contiguous=True),
                 reads=reads, writes=writes, semkey=semkey, inc=16)
        else:
            P.op(q, lambda e: e.dma_start(out=out, in_=in_), reads=reads, writes=writes, semkey=semkey, inc=16)

    def MM(mms, reads, writes):
        mms = list(mms)
        P.pe_log.append((P.phase, len(mms)))

        def fn(e):
            last = None
            for (o, l_, r_, s0, s1) in mms:
                last = e.matmul(o, lhsT=l_, rhs=r_, start=s0, stop=s1)
            return last
        P.op("pe", fn, reads=reads, writes=writes)

    def TR(trs, reads, writes):
        trs = list(trs)
        P.pe_log.append((P.phase, len(trs)))

        def fn(e):
            last = None
            for (o, i_, idn) in trs:
                last = e.transpose(out=o, in_=i_, identity=idn)
            return last
        P.op("pe", fn, reads=reads, writes=writes)

    cp_eng = {"i": 0}

    def alt():
        cp_eng["i"] += 1
        return "act" if cp_eng["i"] % 2 else "dve"

    slabs = {}

    def reg_weight(name, l, src2d, specs):
        tot = len(specs)
        mx = max(nk * ncw for (_, _, nk, _, ncw) in specs)
        scr = nc.dram_tensor("scr_%s_%d" % (name, l), [tot, 128, mx], BF16, kind="Internal").ap()
        for i, (key, row0, nk, col0, ncw) in enumerate(specs):
            src = src2d[row0:row0 + nk * 128, col0:col0 + ncw].rearrange("(k p) c -> p k c", p=128)
            slabs[key] = dict(scr=scr[i, :, 0:nk * ncw], nk=nk, ncw=ncw, src=src, cast=False)

    IN_COLS = [(0, "big", 0), (512, "big", 1), (1024, "big", 2), (1536, "big", 3), (2048, "big", 4),
               (2560, "big", 5), (3080, "big", 6), (3592, "big", 7), (4104, "kvt", None), (3072, "gts", None)]
    for l in range(2):
        reg_weight("win", l, w_in[l], [(("win", l, j), 0, KC, c0, (8 if kind == "gts" else 512))
                                       for j, (c0, kind, _) in enumerate(IN_COLS)])
        reg_weight("wout", l, w_out[l], [(("wout", l, cg), 0, 16, cg * DCW, DCW) for cg in range(NDC)])
        reg_weight("wup", l, w_up[l], [(("wup", l, hf, j), 0, KC, hf * HH + j * 512, 512)
                                       for hf in range(NHALF) for j in range(UPS)])
        reg_weight("wdn", l, w_down[l], [(("wdn", l, hf, cg, s), hf * HH + s * DNK * 128, DNK, cg * DCW, DCW)
                                         for hf in range(NHALF) for cg in range(NDC) for s in range(DNS)])

    passes = []
    for si in range(NSEQ):
        for ti in range(NPASS_SEQ):
            passes.append(("p", si, ti))
    if NS > 0:
        passes.append(("s", 0, 0))
    sched = []
    for ps_ in passes:
        for l in range(2):
            for j in range(len(IN_COLS)):
                sched.append(("win", l, j))
            for cg in range(NDC):
                sched.append(("wout", l, cg))
            for hf in range(NHALF):
                for j in range(UPS):
                    sched.append(("wup", l, hf, j))
                for cg in range(NDC):
                    for s in range(DNS):
                        sched.append(("wdn", l, hf, cg, s))
    wbuf = [sb("wbuf%d" % s, [128, 16 * 512], BF16) for s in range(NW)]
    wslots = [w_[:] for w_ in wbuf]
    per_pass = len(sched) // len(passes)
    extra_slot = (NS > 0) and (4 * D >= 16 * 512) and NW == 2
    j0s = len(sched) - per_pass if NS > 0 else len(sched)

    def slot_of(j):
        if extra_slot and j >= j0s:
            return [j0s % 2, (j0s + 1) % 2, 2][(j - j0s) % 3]
        return j % NW

    def nslots(j):
        return 3 if (extra_slot and j >= j0s) else NW
    wstate = {"loaded": 0, "used": 0, "cast": 0, "ncast": 0}
    CL = 20
    NCS = 32

    def cast_upto(j):
        while wstate["cast"] <= min(j, len(sched) - 1):
            key = sched[wstate["cast"]]
            wstate["cast"] += 1
            sl = slabs[key]
            if sl["cast"]:
                continue
            sl["cast"] = True
            ck_ = "cast%d" % (wstate["ncast"] % NCS)
            wstate["ncast"] += 1
            DMA("pool", sl["scr"].rearrange("p (k c) -> p k c", k=sl["nk"]), sl["src"], [], [("scr", key)], ck_)

    def load_next():
        j = wstate["loaded"]
        key = sched[j]
        sl = slabs[key]
        s = slot_of(j)
        wr = [("w", s)] + ([("x", 2), ("x", 3)] if s == 2 else [])
        DMA("sp", wslots[s][:, 0:sl["nk"] * sl["ncw"]], sl["scr"], [("scr", key)], wr, "wl%d" % s)
        wstate["loaded"] += 1

    def prefetch_extra():
        j = wstate["used"]
        if j >= len(sched):
            return
        cast_upto(j + nslots(j) + CL)
        while wstate["loaded"] < min(len(sched), j + nslots(j)):
            load_next()

    def next_slab(key):
        j = wstate["used"]
        assert sched[j] == key, (sched[j], key)
        cast_upto(j + NW + CL)
        while wstate["loaded"] < min(len(sched), j + nslots(j)):
            load_next()
        wstate["used"] += 1
        sl = slabs[key]
        s = slot_of(j)
        return wslots[s][:, 0:sl["nk"] * sl["ncw"]].rearrange("p (k c) -> p k c", k=sl["nk"]), ("w", s)

    banks = [es.enter_context(nc.psum_tensor("ps%d" % b, [128, 512], F32)) for b in range(8)]
    from collections import deque
    pfree_list = deque(range(7))

    def psum():
        assert pfree_list, "out of PSUM banks"
        b = pfree_list.popleft()
        return banks[b], ("ps", b)

    def pfree(key):
        assert key[1] not in pfree_list
        pfree_list.append(key[1])
    pm = banks[7]
    pmk = ("ps", 7)

    x0_ = sb("x0", [128, D], F32)
    x1_ = sb("x1", [128, D], F32)
    x23 = sb("x23", [128, 2 * D], F32)
    x = [x0_[:], x1_[:], x23[:, 0:D], x23[:, D:2 * D]]
    if extra_slot:
        wslots.append(x23[:].bitcast(BF16)[:, 0:16 * 512])
    actT = sb("actT", [128, 16, 512], BF16)
    big = sb("big", [128, 16384], BF16)
    bigp = big[:].rearrange("p (t c) -> p t c", t=4)
    uT = big[:].rearrange("p (h n) -> p h n", n=512)
    kvt = [sb("kvt%d" % t, [128, 512], F32) for t in range(4)]
    gts = [sb("gts%d" % t, [128, 8], F32) for t in range(4)]
    BT = [sb("BT%d" % t, [128, 16, 128], F32) for t in range(2)]
    xs_t = [sb("xs%d" % i, [128, D], BF16) for i in range(2)]
    identb = sb("identb", [128, 128], BF16)
    identf = sb("identf", [128, 128], F32)
    ones4 = sb("ones4", [4, 128], F32)
    zer4 = sb("zer4", [4, 128], F32)
    selh = sb("selh", [4, 4, 128], F32)
    gcol = sb("gcol", [128, 2, 2, 16], F32)
    gmo_c = sb("gmo_c", [128, 2, 8], F32)
    gao_c = sb("gao_c", [128, 2, 8], F32)
    gq_c = sb("gq_c", [64, 2], F32)
    gk_c = sb("gk_c", [64, 2], F32)
    gq8_c = sb("gq8_c", [64, 2], F32)
    gk_bc = sb("gk_bc", [128, 2, 64], F32)
    bi_c = sb("bi_c", [4, 2], F32)
    nbf_c = sb("nbf_c", [4, 2], F32)
    sinkexp = sb("sinkexp", [128, 32], F32)
    rbx = sb("rbx", [33, 16], F32)
    st = sb("st", [128, 64], F32)
    Cst = [sb("C%d" % l, [128, 4, 257], F32) for l in range(2)]
    Cbf = [sb("Cbf%d" % l, [128, 4, 258], BF16) for l in range(2)]
    kTpp = [[sb("kTpp%d_%d" % (l, i), [64, 4, 128], BF16) for i in range(2)] for l in range(2)]
    vApp = [[sb("vApp%d_%d" % (l, i), [128, 4, 66], BF16) for i in range(2)] for l in range(2)]
    carL = [sb("carL%d" % l, [4, 1], F32) for l in range(2)]
    carM = [sb("carM%d" % l, [4, 1], F32) for l in range(2)]
    g_ig = sb("g_ig", [4, 128], F32); g_l = sb("g_l", [4, 128], F32); g_Lc = sb("g_Lc", [4, 128], F32)
    g_A = sb("g_A", [4, 128], F32); g_mu = sb("g_mu", [4, 128], F32); g_g = sb("g_g", [4, 128], F32)
    g_md = sb("g_md", [4, 128], F32); g_wk = sb("g_wk", [4, 128], F32)
    g_s = sb("g_s", [4, 8], F32)
    dd4 = sb("dd4", [4, 4], F32)
    gpt = sb("gpt", [128, 16], F32)
    decbc = sb("decbc", [128, 4], F32)
    qT = sb("qT", [128, 4, 128], BF16); kT = sb("kT", [128, 4, 128], BF16)
    vext = sb("vext", [128, 4, 258], BF16)
    ET = [sb("ET%d" % i, [128, 128], F32) for i in range(2)]
    maskB = sb("maskB", [128, 128], F32)
    wT = [sb("wT_%d" % i, [128, 128], BF16) for i in range(2)]
    kw = [sb("kw%d" % i, [128, 128], BF16) for i in range(2)]
    inter_s0 = sb("inter_s0", [128, 257], F32)
    TA = sb("TA", [128, 4, 257], F32)
    TAf = TA[:].rearrange("p h c -> p (h c)")
    TB1 = sb("TB1", [128, 1024], BF16)
    TB2 = sb("TB2", [128, 1024], BF16)
    TA2 = sb("TA2", [128, 1024], F32)
    TA2f = TA2[:]
    TB1b = sb("TB1b", [128, 1024], BF16)
    inter_s = [inter_s0[:], TB2[:].bitcast(F32)[:, 0:257]]
    inter_k = [("inter_s", 0), "TB2"]
    TB2b = sb("TB2b", [128, 1024], BF16)
    ks = sb("ks", [128, 256], BF16)
    knf = sb("knf", [128, 256], F32)
    qnT = sb("qnT", [64, 16, 128], BF16)
    sbS = [sb("sbS%d" % i, [128, 512], F32) for i in range(2)]
    PT = [sb("PT%d" % i, [128, 512], BF16) for i in range(2)]

    CH = lambda l: [("C", l, h) for h in range(4)]
    CBH = lambda l: [("Cbf", l, h) for h in range(4)]

    MEMSET("pool", identf[:], 1.0, ["identf"])
    ASEL(identf[:], identf[:], [[-1, 128]], ALU.is_equal, 0, 1, ["identf"], ["identf"])
    TC("pool", identb[:], identf[:], ["identf"], ["identb"])
    MEMSET("pool", ones4[:], 1.0, ["ones4"])
    MEMSET("pool", maskB[:], 30000.0, ["maskB"])
    ASEL(maskB[:], maskB[:], [[-1, 128]], ALU.is_gt, 0, 1, ["maskB"], ["maskB"])
    MEMSET("pool", zer4[:], 0.0, ["zer4"])
    MEMSET("pool", selh[:], 1.0, ["selh"])
    for h in range(4):
        ASEL(selh[:, h, :], selh[:, h, :], [[0, 128]], ALU.is_equal, -h, 1, ["selh"], ["selh"])
    for l in range(2):
        for pi_ in range(2):
            MEMSET("pool", vApp[l][pi_][:, :, 64:65], 1.0, [("vA1", l, pi_)])
    MEMSET("pool", vext[:, :, 256:257], 1.0, ["vext1"])
    MEMSET("pool", rbx[:], NEG, ["rbx"])
    cq = "cst"
    cres = ["gcol", "gmo_c", "gao_c", "gq_c", "gk_c", "gk_bc", "bi_c", "nbf_c", "sinkexp", "rbx"]
    for l in range(2):
        DMA("sp", gcol[:, l, 0, 0:KC], g_mix[l].rearrange("(k p) -> p k", p=128), [], [("cst_tmp", "gcol_mix", l)], cq, slow=True)
        DMA("sp", gcol[:, l, 1, 0:KC], g_ffn[l].rearrange("(k p) -> p k", p=128), [], [("cst_tmp", "gcol", l)], cq, slow=True)
        DMA("sp", gmo_c[:, l, :], g_mo[l].rearrange("(k p) -> p k", p=128), [], [("cst_tmp", "gmo_c", l)], cq, slow=True)
        DMA("sp", gao_c[:, l, :], g_ao[l].rearrange("(k p) -> p k", p=128), [], [("cst_tmp", "gao_c", l)], cq, slow=True)
        DMA("sp", gq_c[:, l:l + 1], g_q[l].rearrange("(p o) -> p o", o=1), [], [("cst_tmp", "gq_c", l)], cq, slow=True)
        DMA("sp", gk_c[:, l:l + 1], g_k[l].rearrange("(p o) -> p o", o=1), [], [("cst_tmp", "gk_c", l)], cq, slow=True)
        DMA("sp", gk_bc[:, l, :], g_k[l:l + 1, :].broadcast_to([128, 64]), [], [("cst_tmp", "gk_bc", l)], cq, slow=True)
        DMA("sp", bi_c[:, l:l + 1], b_i[l].rearrange("(p o) -> p o", o=1), [], [("cst_tmp", "bi_c", l)], cq, slow=True)
        DMA("sp", nbf_c[:, l:l + 1], b_f[l].rearrange("(p o) -> p o", o=1), [], [("cst_tmp", "nbf_c", l)], cq, slow=True)
        DMA("sp", sinkexp[:, l * 16:(l + 1) * 16], sinks[l:l + 1, :].broadcast_to([128, 16]), [], [("cst_tmp", "sinkexp", l)], cq,
            slow=True)
    DMA("sp", rbx[0:32, :], rel_bias, ["rbx"], [("cst_tmp", "rbx", 0)], cq)
    for r in cres:
        P.lastw[r] = (cq, P.count[cq])
    ACT(sinkexp[:], sinkexp[:], AF.Exp, ["sinkexp"], ["sinkexp"])
    TS("dve", gq8_c[:], gq_c[:], 0.125, None, ALU.mult, None, ["gq_c"], ["gq8_c"])
    TS("dve", nbf_c[:], nbf_c[:], -1.0, None, ALU.mult, None, ["nbf_c"], ["nbf_c"])
    for tbl in range(2):
        for i0 in range(0, 128, 32):
            pb, pbk = psum()
            for sub in range(8):
                bi_ = sub % 2
                buf = sbS[bi_][0:33, :].rearrange("p (a k) -> p a k", a=4)
                bk = ("sbS", bi_)
                DMA("sp", buf, oh[tbl, :, i0 + sub * 4:i0 + sub * 4 + 4, :], [], [bk], "ohl%d" % bi_)
                MM([(pb[:, (sub * 4 + ii) * 16:(sub * 4 + ii + 1) * 16], buf[:, ii, :], rbx[:, :], True, True)
                    for ii in range(4)], [bk, "rbx"], [pbk])
            TC("dve", BT[tbl][:, :, i0:i0 + 32].rearrange("p h i -> p i h"),
               pb[:, :].rearrange("p (i h) -> p i h", h=16), [pbk], [("BT", tbl)])
            pfree(pbk)

    def rmsnorm_all(tiles, l, which):
        ntt = len(tiles)
        n = tiles[0]
        for tt in range(ntt):
            xs = xs_t[tt % 2]
            ACT(xs[:n, :], x[tt][:n, :], AF.Square, [("x", tt)], ["st_n0", ("xs", tt % 2)], accum=st[:n, tt:tt + 1])
        ACT(st[:n, 0:ntt], st[:n, 0:ntt], AF.Ln, ["st_n0"], ["st_n0"], scale=1.0 / D, bias=EPS)
        ACT(st[:n, 4:4 + ntt], st[:n, 0:ntt], AF.Exp, ["st_n0"], ["st_n3"], scale=-0.5)
        for tt in range(ntt):
            xs = xs_t[tt % 2]
            xk = ("xs", tt % 2)
            ACT(xs[:n, :], x[tt][:n, :], AF.Copy, [("x", tt), "st_n3"], [xk], scale=st[:n, 4 + tt:5 + tt])
            for f0 in range(0, KC, 4):
                nf = min(4, KC - f0)
                pb, pbk = psum()
                pbb = pb[:].bitcast(BF16)
                TR([(pbb[:, j * 128:j * 128 + n], xs[:n, (f0 + j) * 128:(f0 + j + 1) * 128], identb[:n, :n])
                    for j in range(nf)], [xk, "identb"], [pbk])
                TT("dve", actT[:, f0:f0 + nf, tt * 128:tt * 128 + n],
                   pbb[:, 0:nf * 128].rearrange("p (k c) -> p k c", k=nf)[:, :, 0:n],
                   gcol[:, l, which, f0:f0 + nf].unsqueeze(2).broadcast_to([128, nf, n]), ALU.mult,
                   [pbk, "gcol"], [("aT", tt, 0), ("aT", tt, 1)])
                pfree(pbk)

    def prompt_state_init(l):
        MEMSET("pool", Cst[l][:], 0.0, CH(l))
        MEMSET("pool", Cbf[l][:], 0.0, CBH(l))
        MEMSET("pool", carL[l][:], 0.0, [("carL", l)])
        MEMSET("pool", carM[l][:], 0.0, [("carM", l)])

    TAH = [("TA", h) for h in range(4)]
    TA2H = [("TA2", h) for h in range(4)]

    def sample_mlstm_load(l, j):
        stg = TAf[:, 0:1024].rearrange("p (a k) -> p a k", a=8)
        DMA("pool", stg, sC[l, j].rearrange("h (c p) k -> p (h c) k", p=128), [], TAH, "sldC")
        yield
        for h0 in range(0, 4, 2):
            pb, pbk = psum()
            TR([(pb[:, (hh * 2 + c2) * 128:(hh * 2 + c2 + 1) * 128], stg[:, (h0 + hh) * 2 + c2, :], identf[:])
                for hh in range(2) for c2 in range(2)], TAH + ["identf"], [pbk])
            TC("dve", Cst[l][:, h0:h0 + 2, 0:256], pb[:, :].rearrange("p (h v) -> p h v", h=2), [pbk], CH(l))
            pfree(pbk)
            yield
        DMA("pool", Cst[l][:, :, 256:257].rearrange("p h o -> p (h o)"), sn[l, j].rearrange("h k -> k h"),
            CH(l), CH(l), "sldn", slow=True)
        ACT(Cbf[l][:, :, 0:257], Cst[l][:], AF.Copy, CH(l), CBH(l))
        MEMSET("pool", carL[l][:], 0.0, [("carL", l)])
        DMA("pool", carM[l][:], sm[l, j].rearrange("(p o) -> p o", o=1), [], [("carM", l)], "sldm", slow=True)
        yield

    def sample_attn_load(l, j):
        stk = TA2f[:, 0:256]
        DMA("pool", stk, ck[l, j], TA2H, TA2H, "sldk")
        TC("dve", ks[:, :], stk, TA2H, ["ks"])
        yield
        pb, pbk = psum()
        pbb = pb[:].bitcast(BF16)
        TR([(pbb[0:64, kh * 128:(kh + 1) * 128], ks[:, kh * 64:(kh + 1) * 64], identb[:]) for kh in range(4)],
           ["ks", "identb"], [pbk])
        ACT(kTpp[l][0][:], pbb[0:64, 0:512].rearrange("p (h n) -> p h n", h=4), AF.Copy, [pbk], [("kTpp", l, 0)])
        pfree(pbk)
        yield
        stv = TA2f[:, 256:512]
        DMA("pool", stv, cv[l, j], TA2H, TA2H, "sldv")
        TC("dve", vApp[l][0][:, :, 0:64], stv.rearrange("p (h d) -> p h d", h=4), TA2H + [("vA1", l, 0)], [("vA", l, 0)])
        DMA("pool", s_k[l, j, 0:128 - LS, :], ck[l, j, LS:128, :], [], [], "so_cp")
        DMA("pool", s_v[l, j, 0:128 - LS, :], cv[l, j, LS:128, :], [], [], "so_cp")
        yield

    def mlstm_gates(l, tt, n, fin):
        cL = ("carL", l); cM = ("carM", l)
        pg, pgk = psum()
        TR([(pg[0:4, 0:n], gts[tt][:n, 0:4], identf[:n, :n]),
            (pg[0:4, 128:128 + n], gts[tt][:n, 4:8], identf[:n, :n])], [("gts", tt), "identf"], [pgk])
        ACT(g_ig[:, :n], pg[0:4, 0:n], AF.Identity, [pgk, "bi_c"], ["g_ig"], bias=bi_c[:, l:l + 1])
        ACT(g_l[:, :n], pg[0:4, 128:128 + n], AF.Exp, [pgk, "nbf_c"], ["g_l"], scale=-1.0, bias=nbf_c[:, l:l + 1])
        pfree(pgk)
        yield
        ACT(g_l[:, :n], g_l[:, :n], AF.Ln, ["g_l"], ["g_l"], bias=1.0)
        SCAN(g_Lc[:, :n], g_l[:, :n], zer4[:, :n], carL[l][:, 0:1], ALU.add, ALU.add, ["g_l", "zer4", cL], ["g_Lc"])
        yield
        TT("dve", g_A[:, :n], g_ig[:, :n], g_Lc[:, :n], ALU.add, ["g_ig", "g_Lc"], ["g_A"])
        SCAN(g_mu[:, :n], g_A[:, :n], zer4[:, :n], carM[l][:, 0:1], ALU.max, ALU.add, ["g_A", "zer4", cM], ["g_mu"])
        yield
        ACT(g_g[:, :n], g_mu[:, :n], AF.Exp, ["g_mu", cM], ["g_g"], scale=-1.0, bias=carM[l][:, 0:1])
        TT("dve", g_md[:, :n], g_Lc[:, :n], g_mu[:, :n], ALU.subtract, ["g_Lc", "g_mu"], ["g_md"])
        yield
        ACT(g_md[:, :n], g_md[:, :n], AF.Exp, ["g_md"], ["g_md"])
        TS("dve", g_s[:, 0:1], g_mu[:, n - 1:n], -1.0, None, ALU.mult, None, ["g_mu"], ["g_s0"])
        yield
        ACT(g_wk[:, :n], g_A[:, :n], AF.Exp, ["g_A", "g_s0"], ["g_wk"], bias=g_s[:, 0:1])
        ACT(g_s[:, 1:2], carM[l][:, 0:1], AF.Exp, [cM, "g_s0"], ["g_s1"], bias=g_s[:, 0:1])
        if fin:
            TT("dve", g_s[:, 2:3], g_mu[:, n - 1:n], g_Lc[:, n - 1:n], ALU.subtract, ["g_mu", "g_Lc"], ["g_s2"])
        TC("dve", carL[l][:, 0:1], g_Lc[:, n - 1:n], ["g_Lc"], [cL])
        TC("dve", carM[l][:, 0:1], g_mu[:, n - 1:n], ["g_mu"], [cM])
        yield

    def mlstm_tile(l, tt, n, fin, kind, si, side=None):
        mqv = bigp[:, tt, 0:512]; mkv = bigp[:, tt, 512:1024]
        mvv = bigp[:, tt, 1024:2048]; mov = bigp[:, tt, 2048:3072]
        rbig = [("big", tt * 8 + i) for i in range(6)]
        sd = [side]

        def tick():
            if sd[0] is not None:
                try:
                    next(sd[0])
                except StopIteration:
                    sd[0] = None
        pt, ptk = psum()
        TR([(pt[:n, 0:4], g_A[:, :n], identf[0:4, 0:4]), (pt[:n, 4:8], g_g[:, :n], identf[0:4, 0:4]),
            (pt[:n, 8:12], g_md[:, :n], identf[0:4, 0:4]), (pt[:n, 12:16], g_wk[:, :n], identf[0:4, 0:4])],
           ["g_A", "g_g", "g_md", "g_wk", "identf"], [ptk])
        TC("dve", gpt[:n, :], pt[:n, 0:16], [ptk], ["gpt"])
        pfree(ptk)
        TS("dve", dd4[:], identf[0:4, 0:4], g_s[:, 1:2], None, ALU.mult, None, ["identf", "g_s1"], ["dd4"])
        yield
        pd, pdk = psum()
        MM([(pd[:, 0:4], ones4[:, :], dd4[:, :], True, True)], ["ones4", "dd4"], [pdk])
        TC("dve", decbc[:], pd[:, 0:4], [pdk], ["decbc"])
        pfree(pdk)
        mm_ = []
        for h in range(4):
            mm_.append((pm[:, h * 128:h * 128 + n], selh[:, h, :], g_mu[:, :n], True, False))
            mm_.append((pm[:n, h * 128:h * 128 + n], identf[:n, :n], maskB[:n, :n], False, True))
        MM(mm_, ["selh", "g_mu", "identf", "maskB"], [pmk])
        yield
        pq, pqk = psum()
        pqb = pq[:].bitcast(BF16)
        TR([(pqb[:, h * 128:h * 128 + n], mqv[:n, h * 128:(h + 1) * 128], identb[:n, :n]) for h in range(4)] +
           [(pqb[:, 512 + h * 128:512 + h * 128 + n], mkv[:n, h * 128:(h + 1) * 128], identb[:n, :n])
            for h in range(4)], rbig[0:2] + ["identb"], [pqk])
        ACT(qT[:, :, :n], pqb[:, 0:512].rearrange("p (h c) -> p h c", h=4)[:, :, :n], AF.Copy, [pqk], ["qT"])
        ACT(kT[:, :, :n], pqb[:, 512:1024].rearrange("p (h c) -> p h c", h=4)[:, :, :n], AF.Copy, [pqk], ["kT"])
        pfree(pqk)
        tick()
        yield
        TC("act", vext[:n, :, 0:256], mvv[:n, :].rearrange("p (h v) -> p h v", h=4), rbig[2:4] + ["vext1"], ["vext"])
        ACT(TB1[:n, :], mov[:n, :], AF.Exp, rbig[4:6], ["TB1"], scale=-1.0)
        TS("dve", TB1[:n, :], TB1[:n, :], 1.0, None, ALU.add, None, ["TB1"], ["TB1"])
        RECIP_LP(TB1[:n, :], TB1[:n, :], ["TB1"], ["TB1"])
        tick()
        yield
        def head_steps(h):
            i2 = h % 2
            pS, pSk = psum()
            MM([(pS[:n, 0:n], kT[:, h, :n], qT[:, h, :n], True, True)], ["kT", "qT"], [pSk])
            ACT(ET[i2][:n, :n], pm[:n, h * 128:h * 128 + n], AF.Exp, [pmk, "gpt"], [("ET", i2)], scale=-1.0,
                bias=gpt[:n, h:h + 1])
            yield
            STT(wT[i2][:n, :n], pS[:n, 0:n], SCL, ET[i2][:n, :n], ALU.mult, ALU.mult, [pSk, ("ET", i2)], [("wT", i2)])
            pfree(pSk)
            TS("dve", kw[i2][:n, :], mkv[:n, h * 128:(h + 1) * 128], gpt[:n, 12 + h:13 + h], SCL, ALU.mult, ALU.mult,
               [rbig[1], "gpt"], [("kw", i2)])
            yield
            pJ, pJk = psum()
            MM([(pJ[:n, 0:257], qT[:, h, :n], Cbf[l][:, h, 0:257], True, True)], ["qT", ("Cbf", l, h)], [pJk])
            ACT(inter_s[i2][:n, :], pJ[:n, 0:257], AF.Copy, [pJk, "gpt"], [inter_k[i2]], scale=gpt[:n, 4 + h:5 + h])
            pfree(pJk)
            yield
            pI, pIk = psum()
            MM([(pI[:n, 0:257], wT[i2][:n, :n], vext[:n, h, 0:257], True, True)], [("wT", i2), "vext"], [pIk])
            TT("dve", TA[:n, h, :], pI[:n, 0:257], inter_s[i2][:n, :], ALU.add, [pIk, inter_k[i2]], [("TA", h)])
            pfree(pIk)
            yield
            pC, pCk = psum()
            MM([(pC[:, 0:257], kw[i2][:n, :], vext[:n, h, 0:257], True, True)], [("kw", i2), "vext"], [pCk])
            STT(Cst[l][:, h, :], Cst[l][:, h, :], decbc[:, h:h + 1], pC[:, 0:257], ALU.mult, ALU.add,
                [pCk, "decbc", ("C", l, h)], [("C", l, h)])
            pfree(pCk)
            TC("pool", Cbf[l][:, h, 0:257], Cst[l][:, h, :], [("C", l, h)], [("Cbf", l, h)])
            yield

        for hp in ((0, 1), (2, 3)):
            ga, gb = head_steps(hp[0]), head_steps(hp[1])
            alive = [ga, gb]
            while alive:
                for g_ in list(alive):
                    try:
                        next(g_)
                    except StopIteration:
                        alive.remove(g_)
                tick()
                yield
        hn_den = TA[:n, :, 256:257].rearrange("p h o -> p (h o)")
        ACT(st[:n, 8:12], hn_den, AF.Abs, TAH, ["st_m0"])
        TT("dve", st[:n, 8:12], st[:n, 8:12], gpt[:n, 8:12], ALU.max, ["st_m0", "gpt"], ["st_m0"])
        tick()
        yield
        RECIP(st[:n, 12:16], st[:n, 8:12], ["st_m0"], ["st_m1"])
        for h in range(4):
            ACT(TB2[:n, h * 256:(h + 1) * 256], TA[:n, h, 0:256], AF.Square, [("TA", h)], [("st_m2", h), "TB2"],
                accum=st[:n, 16 + h:17 + h])
            if h % 2:
                tick()
                yield
        TT("dve", st[:n, 20:24], st[:n, 12:16], st[:n, 12:16], ALU.mult, ["st_m1"], ["st_m3"])
        TT("dve", st[:n, 20:24], st[:n, 20:24], st[:n, 16:20], ALU.mult, ["st_m3"] + [("st_m2", h) for h in range(4)],
           ["st_m3"])
        tick()
        yield
        ACT(st[:n, 24:28], st[:n, 20:24], AF.Ln, ["st_m3"], ["st_m4"], scale=1.0 / 256, bias=EPS)
        ACT(st[:n, 28:32], st[:n, 24:28], AF.Exp, ["st_m4"], ["st_m5"], scale=-0.5)
        tick()
        yield
        TT("dve", st[:n, 28:32], st[:n, 28:32], st[:n, 12:16], ALU.mult, ["st_m5", "st_m1"], ["st_m5"])
        tick()
        yield
        for h in range(4):
            STT(TB2[:n, h * 256:(h + 1) * 256], TA[:n, h, 0:256], st[:n, 28 + h:29 + h], TB1[:n, h * 256:(h + 1) * 256],
                ALU.mult, ALU.mult, [("TA", h), "st_m5", "TB1"], ["TB2"])
            if h % 2:
                tick()
                yield
        ph, phk = psum()
        phb = ph[:].bitcast(BF16)
        TR([(phb[:, fc * 128:fc * 128 + n], TB2[:n, fc * 128:(fc + 1) * 128], identb[:n, :n]) for fc in range(8)],
           ["TB2", "identb"], [phk])
        TT("dve", actT[:, 0:8, tt * 128:tt * 128 + n], phb[:, 0:1024].rearrange("p (k c) -> p k c", k=8)[:, :, 0:n],
           gmo_c[:, l, :].unsqueeze(2).broadcast_to([128, 8, n]), ALU.mult, [phk, "gmo_c"], [("aT", tt, 0)])
        pfree(phk)
        tick()
        yield
        if fin:
            b = si if kind == "p" else tt
            oC, on_, om = (p_C, p_n, p_m) if kind == "p" else (s_C, s_n, s_m)
            sk_ = "so_m_%s%d_%d" % (kind, l, b)
            stg = TAf[:, 0:1024].rearrange("p (a k) -> p a k", a=8)
            for h0 in range(0, 4, 2):
                pb, pbk = psum()
                TR([(pb[:, (hh * 2 + c2) * 128:(hh * 2 + c2 + 1) * 128], Cst[l][:, h0 + hh, c2 * 128:(c2 + 1) * 128],
                     identf[:]) for hh in range(2) for c2 in range(2)], CH(l) + ["identf"], [pbk])
                TC("dve", stg[:, h0 * 2:h0 * 2 + 4, :], pb[:, :].rearrange("p (a k) -> p a k", a=4), [pbk], TAH)
                pfree(pbk)
                tick()
                yield
            DMA("pool", oC[l, b].rearrange("h (c p) k -> p (h c) k", p=128), stg, TAH, [], sk_)
            DMA("pool", on_[l, b].rearrange("h k -> k h"), Cst[l][:, :, 256:257].rearrange("p h o -> p (h o)"),
                CH(l), [], sk_, slow=True)
            DMA("pool", om[l, b].rearrange("(p o) -> p o", o=1), g_s[:, 2:3], ["g_s2"], [], sk_, slow=True)
            tot = (sk_, P.count.get(sk_, 0))
            for r in TAH + CH(l) + ["g_s2"]:
                P.readers.setdefault(r, []).append(tot)
            tick()
            yield
        while sd[0] is not None:
            tick()
            yield

    def attn_tile(l, tt, n, has_prev, par, fin, kind, si):
        aqv = bigp[:, tt, 3072:4096]
        raq = [("big", tt * 8 + 6), ("big", tt * 8 + 7)]
        kcur = kTpp[l][par]; kprev = kTpp[l][1 - par]
        vcur = vApp[l][par]; vprev = vApp[l][1 - par]
        kck = ("kTpp", l, par); kpk = ("kTpp", l, 1 - par)
        vck = ("vA", l, par); vpk = ("vA", l, 1 - par)
        npv = 128
        g4 = lambda ap2, m: ap2.rearrange("p (g c) -> p g c", g=m)
        ACT(TA2f[:n, 0:1024], aqv[:n, :], AF.Square, raq, TA2H)
        REDUCE(st[:n, 32:48], g4(TA2f[:n, 0:1024], 16), TA2H, ["st_a0"])
        yield
        ACT(TA2f[:n, 0:256], kvt[tt][:n, 0:256], AF.Square, [("kvt", tt)], TA2H)
        REDUCE(st[:n, 48:52], g4(TA2f[:n, 0:256], 4), TA2H, ["st_a1"])
        yield
        ACT(st[:n, 32:52], st[:n, 32:52], AF.Ln, ["st_a0", "st_a1"], ["st_a2"], scale=1.0 / 64, bias=EPS)
        ACT(st[:n, 32:52], st[:n, 32:52], AF.Exp, ["st_a2"], ["st_a2"], scale=-0.5)
        yield
        TT("dve", g4(TB1b[:n, :], 16), g4(aqv[:n, :], 16), st[:n, 32:48].unsqueeze(2).broadcast_to([n, 16, 64]),
           ALU.mult, raq + ["st_a2"], ["TB1b"])
        yield
        TT("dve", g4(knf[:n, :], 4), g4(kvt[tt][:n, 0:256], 4), st[:n, 48:52].unsqueeze(2).broadcast_to([n, 4, 64]),
           ALU.mult, [("kvt", tt), "st_a2"], ["knf"])
        TC("dve", ks[:n, :], knf[:n, :], ["knf"], ["ks"])
        yield
        for h0 in (0, 8):
            pb, pbk = psum()
            pbb = pb[:].bitcast(BF16)
            TR([(pbb[0:64, j * 128:j * 128 + n], TB1b[:n, (h0 + j) * 64:(h0 + j + 1) * 64], identb[:n, :n])
                for j in range(8)], ["TB1b", "identb"], [pbk])
            ACT(qnT[:, h0:h0 + 8, :n], g4(pbb[0:64, 0:1024], 8)[:, :, :n], AF.Copy, [pbk, "gq8_c"], ["qnT"],
                scale=gq8_c[:, l:l + 1])
            pfree(pbk)
            yield
        pb, pbk = psum()
        pbb = pb[:].bitcast(BF16)
        TR([(pbb[0:64, j * 128:j * 128 + n], ks[:n, j * 64:(j + 1) * 64], identb[:n, :n]) for j in range(4)],
           ["ks", "identb"], [pbk])
        ACT(kcur[:, :, :n], g4(pbb[0:64, 0:512], 4)[:, :, :n], AF.Copy, [pbk, "gk_c"], [kck], scale=gk_c[:, l:l + 1])
        pfree(pbk)
        TC("pool", vcur[:n, :, 0:64], g4(kvt[tt][:n, 256:512], 4), [("kvt", tt), ("vA1", l, par)], [vck])
        yield
        if fin:
            b = si if kind == "p" else tt
            ok_, ov_ = (p_k, p_v) if kind == "p" else (s_k, s_v)
            r0 = 0 if kind == "p" else 128 - n
            sk_ = "so_a_%s%d_%d" % (kind, l, b)
            TT("dve", g4(knf[:n, :], 4), g4(knf[:n, :], 4), gk_bc[:n, l, :].unsqueeze(1).broadcast_to([n, 4, 64]),
               ALU.mult, ["knf", "gk_bc"], ["knf"])
            DMA("pool", ok_[l, b, r0:r0 + n, :], knf[:n, :], ["knf"], [], sk_)
            DMA("pool", ov_[l, b, r0:r0 + n, :], kvt[tt][:n, 256:512], [("kvt", tt)], [], sk_)
            tot = (sk_, P.count.get(sk_, 0))
            for r in ["knf", ("kvt", tt)]:
                P.readers.setdefault(r, []).append(tot)
            yield
        for kvh in range(4):
            hs = slice(kvh * 4, kvh * 4 + 4)
            if has_prev:
                pS, pSk = psum()
                MM([(g4(pS[:npv, 0:512], 4)[:, :, :n], kprev[:, kvh, :npv], qnT[:, hs, :n], True, True)],
                   [kpk, "qnT"], [pSk])
                TT("dve", g4(sbS[0][:npv, :], 4)[:, :, :n], g4(pS[:npv, 0:512], 4)[:, :, :n], BT[0][:npv, hs, :n],
                   ALU.add, [pSk, ("BT", 0)], [("sbS", 0)])
                pfree(pSk)
                ACT(g4(PT[0][:npv, :], 4)[:, :, :n], g4(sbS[0][:npv, :], 4)[:, :, :n], AF.Exp, [("sbS", 0)], [("PT", 0)])
                yield
            pS2, pS2k = psum()
            MM([(g4(pS2[:n, 0:512], 4)[:, :, :n], kcur[:, kvh, :n], qnT[:, hs, :n], True, True)], [kck, "qnT"], [pS2k])
            TT("dve", g4(sbS[1][:n, :], 4)[:, :, :n], g4(pS2[:n, 0:512], 4)[:, :, :n], BT[1][:n, hs, :n], ALU.add,
               [pS2k, ("BT", 1)], [("sbS", 1)])
            pfree(pS2k)
            ACT(g4(PT[1][:n, :], 4)[:, :, :n], g4(sbS[1][:n, :], 4)[:, :, :n], AF.Exp, [("sbS", 1)], [("PT", 1)])
            yield
            po, pok = psum()
            pov = po[:, 0:260].rearrange("p (g c) -> p g c", g=4)
            mms = []
            for g in range(4):
                if has_prev:
                    mms.append((pov[:n, g, :], PT[0][:npv, g * 128:g * 128 + n], vprev[:npv, kvh, 0:65], True, False))
                mms.append((pov[:n, g, :], PT[1][:n, g * 128:g * 128 + n], vcur[:n, kvh, 0:65], not has_prev, True))
            MM(mms, [("PT", 0), ("PT", 1), vck, vpk, ("vA1", l, 0), ("vA1", l, 1)], [pok])
            TT("dve", st[:n, 52:56], pov[:n, :, 64:65].rearrange("p g o -> p (g o)"),
               sinkexp[:n, l * 16 + kvh * 4:l * 16 + kvh * 4 + 4], ALU.add, [pok, "sinkexp"], ["st_a3"])
            yield
            RECIP(st[:n, 56:60], st[:n, 52:56], ["st_a3"], ["st_a4"])
            TT("dve", g4(TA2f[:n, kvh * 256:(kvh + 1) * 256], 4), pov[:n, :, 0:64],
               st[:n, 56:60].unsqueeze(2).broadcast_to([n, 4, 64]), ALU.mult, [pok, "st_a4"], [("TA2", kvh)])
            pfree(pok)
            yield
        ACT(TB2b[:n, 0:1024], TA2f[:n, 0:1024], AF.Square, TA2H, ["st_a5", "TB2b"], accum=st[:n, 60:61])
        ACT(st[:n, 62:63], st[:n, 60:61], AF.Ln, ["st_a5"], ["st_a7"], scale=1.0 / 1024, bias=EPS)
        yield
        ACT(st[:n, 63:64], st[:n, 62:63], AF.Exp, ["st_a7"], ["st_a8"], scale=-0.5)
        yield
        ACT(TB2b[:n, :], TA2f[:n, 0:1024], AF.Copy, TA2H + ["st_a8"], ["TB2b"], scale=st[:n, 63:64])
        ph, phk = psum()
        phb = ph[:].bitcast(BF16)
        TR([(phb[:, fc * 128:fc * 128 + n], TB2b[:n, fc * 128:(fc + 1) * 128], identb[:n, :n]) for fc in range(8)],
           ["TB2b", "identb"], [phk])
        TT("dve", actT[:, 8:16, tt * 128:tt * 128 + n], phb[:, 0:1024].rearrange("p (k c) -> p k c", k=8)[:, :, 0:n],
           gao_c[:, l, :].unsqueeze(2).broadcast_to([128, 8, n]), ALU.mult, [phk, "gao_c"], [("aT", tt, 1)])
        pfree(phk)
        yield

    def interleave(gens):
        gens = list(gens)
        while gens:
            for g in list(gens):
                try:
                    next(g)
                except StopIteration:
                    gens.remove(g)

    def run_pass(pinfo):
        kind, si, ti = pinfo
        if kind == "p":
            tiles = [128] * 4
            first = (ti == 0)
            last = (ti == NPASS_SEQ - 1)
        else:
            tiles = [LS] * NS
            first = True
            last = True
        ntt = len(tiles)
        full = all(n == 128 for n in tiles)

        for tt, n in enumerate(tiles):
            src = xp[si, ti * T + tt * 128: ti * T + tt * 128 + n, :] if kind == "p" else xsm[tt, 0:n, :]
            DMA("pool", x[tt][:n, :], src, [], [("x", tt)], "xl%d" % tt)

        for l in range(2):
            P.marks.append(("norm1", pinfo, l, P.nops)); P.phase = "norm1"
            prefetch_extra()
            rmsnorm_all(tiles, l, 0)
            P.marks.append(("inproj", pinfo, l, P.nops)); P.phase = "inproj"
            for j, (c0, knd, cgi) in enumerate(IN_COLS):
                wv, wk_ = next_slab(("win", l, j))
                ncw = 8 if knd == "gts" else 512
                for tt, n in enumerate(tiles):
                    pb, pbk = psum()
                    MM([(pb[:n, 0:ncw], actT[:, kc, tt * 128:tt * 128 + n], wv[:, kc, :], kc == 0, kc == KC - 1)
                        for kc in range(KC)], [wk_, ("aT", tt, 0), ("aT", tt, 1)], [pbk])
                    if knd == "big":
                        TC(alt(), bigp[:n, tt, cgi * 512:(cgi + 1) * 512], pb[:n, :], [pbk], [("big", tt * 8 + cgi)])
                    elif knd == "gts":
                        TC("dve", gts[tt][:n, :], pb[:n, 0:8], [pbk], [("gts", tt)])
                    else:
                        TC(alt(), kvt[tt][:n, :], pb[:n, :], [pbk], [("kvt", tt)])
                    pfree(pbk)
            P.marks.append(("mix", pinfo, l, P.nops)); P.phase = "mix"
            prefetch_extra()

            def stream_m():
                for tt, n in enumerate(tiles):
                    if kind == "s":
                        yield from sample_mlstm_load(l, tt)
                    elif first and tt == 0:
                        prompt_state_init(l)
                    fin = last and (tt == ntt - 1 or kind == "s")
                    if kind == "s" or tt == 0:
                        yield from mlstm_gates(l, tt, n, fin)
                    side = None
                    if kind == "p" and tt + 1 < ntt:
                        fin_n = last and (tt + 1 == ntt - 1)
                        side = mlstm_gates(l, tt + 1, tiles[tt + 1], fin_n)
                    yield from mlstm_tile(l, tt, n, fin, kind, si, side)

            def stream_a():
                for tt, n in enumerate(tiles):
                    if kind == "s":
                        yield from sample_attn_load(l, tt)
                    has_prev = not (kind == "p" and first and tt == 0)
                    par = (ti * 4 + tt) % 2 if kind == "p" else 1
                    fin = last and (tt == ntt - 1 or kind == "s")
                    yield from attn_tile(l, tt, n, has_prev, par, fin, kind, si)

            interleave([stream_m(), stream_a()])
            P.marks.append(("outproj", pinfo, l, P.nops)); P.phase = "outproj"
            for cg in range(NDC):
                wv, wk_ = next_slab(("wout", l, cg))
                for tt, n in enumerate(tiles):
                    pb, pbk = psum()
                    MM([(pb[:n, 0:DCW], actT[:, fc, tt * 128:tt * 128 + n], wv[:, fc, :], fc == 0, fc == 15)
                        for fc in range(16)], [wk_, ("aT", tt, 0), ("aT", tt, 1)], [pbk])
                    xv = x[tt][:n, cg * DCW:(cg + 1) * DCW]
                    TT("dve", xv, pb[:n, 0:DCW], xv, ALU.add, [pbk, ("x", tt)], [("x", tt)])
                    pfree(pbk)
            P.marks.append(("norm2", pinfo, l, P.nops)); P.phase = "norm2"
            prefetch_extra()
            rmsnorm_all(tiles, l, 1)
            P.marks.append(("ffn", pinfo, l, P.nops)); P.phase = "ffn"
            aT_reads = [("aT", tt, h_) for tt in range(ntt) for h_ in range(2)]
            for hf in range(NHALF):
                for j in range(UPS):
                    wv, wk_ = next_slab(("wup", l, hf, j))
                    for h4 in range(4):
                        hc = j * 4 + h4
                        pb, pbk = psum()
                        rk = ("sbS", hc % 2)
                        if full:
                            segs = [(0, ntt * 128)]
                        else:
                            segs = [(tt * 128, tiles[tt]) for tt in range(ntt)]
                        for (c0_, nn_) in segs:
                            MM([(pb[:, c0_:c0_ + nn_], wv[:, kc, h4 * 128:(h4 + 1) * 128], actT[:, kc, c0_:c0_ + nn_],
                                 kc == 0, kc == KC - 1) for kc in range(KC)], [wk_] + aT_reads, [pbk])
                        for (c0_, nn_) in segs:
                            ACT(sbS[hc % 2][:, c0_:c0_ + nn_], pb[:, c0_:c0_ + nn_], AF.Relu, [pbk], [rk])
                        pfree(pbk)
                        for (c0_, nn_) in segs:
                            rv = sbS[hc % 2][:, c0_:c0_ + nn_]
                            TT("pool", uT[:, hc, c0_:c0_ + nn_], rv, rv, ALU.mult, [rk], [("big", hc)])
                for cg in range(NDC):
                    pbs = [psum() for _ in tiles]
                    for s in range(DNS):
                        wv, wk_ = next_slab(("wdn", l, hf, cg, s))
                        for tt, n in enumerate(tiles):
                            pb, pbk = pbs[tt]
                            MM([(pb[:n, 0:DCW], uT[:, s * DNK + hc, tt * 128:tt * 128 + n], wv[:, hc, :],
                                 s == 0 and hc == 0, s == DNS - 1 and hc == DNK - 1) for hc in range(DNK)],
                               [wk_] + [("big", s * DNK + hc) for hc in range(DNK)], [pbk])
                    for tt, n in enumerate(tiles):
                        pb, pbk = pbs[tt]
                        xv = x[tt][:n, cg * DCW:(cg + 1) * DCW]
                        TT("dve", xv, pb[:n, 0:DCW], xv, ALU.add, [pbk, ("x", tt)], [("x", tt)])
                        pfree(pbk)
        for tt, n in enumerate(tiles):
            dst = y_p[si, ti * T + tt * 128: ti * T + tt * 128 + n, :] if kind == "p" else y_s[tt, 0:n, :]
            DMA("pool", dst, x[tt][:n, :], [("x", tt)], [], "so_y%d" % tt)

    P.marks.append(("const_end", P.nops))
    for pinfo in passes:
        run_pass(pinfo)
    assert P.max_ops or wstate["used"] == len(sched)
    P.wait_all("sp", lambda k: k.startswith("so_"))
    print("sbuf bytes remaining", nc.sbuf_bytes_remaining if not callable(nc.sbuf_bytes_remaining) else nc.sbuf_bytes_remaining())
    P.emit()
    es.close()
    return nc, P


_CACHE = {}


def _get_program(cfg):
    key = tuple(sorted(cfg.items()))
    if key not in _CACHE:
        _CACHE[key] = build_program(cfg)
    return _CACHE[key][0]


def run_cores(cfg, ncores, inputs):
    NSEQ = cfg["NSEQ"]; NS = cfg["NS"]
    nc = _get_program(cfg)
    oh = make_onehot()
    f = lambda a: np.ascontiguousarray(np.asarray(a, dtype=np.float32))
    shared = {k: f(inputs[k]) for k in ("rel_bias", "g_mix", "w_in", "b_i", "b_f", "g_q", "g_k", "sinks", "g_mo",
                                        "g_ao", "w_out", "g_ffn", "w_up", "w_down")}
    shared["oh"] = oh
    xp = f(inputs["x_prompt"]); xs = f(inputs["x_sample"])
    ck = f(inputs["cache_k"]); cv = f(inputs["cache_v"])
    sC = f(inputs["state_C"]); sn = f(inputs["state_n"]); sm = f(inputs["state_m"])
    in_maps = []
    for c in range(ncores):
        ps = slice(c * NSEQ, (c + 1) * NSEQ); ss = slice(c * NS, (c + 1) * NS)
        m = dict(shared)
        m["xp"] = np.ascontiguousarray(xp[ps]); m["xs"] = np.ascontiguousarray(xs[ss])
        m["ck"] = np.ascontiguousarray(ck[:, ss].reshape(2, NS, 128, 256))
        m["cv"] = np.ascontiguousarray(cv[:, ss].reshape(2, NS, 128, 256))
        m["sC"] = np.ascontiguousarray(sC[:, ss]); m["sn"] = np.ascontiguousarray(sn[:, ss])
        m["sm"] = np.ascontiguousarray(sm[:, ss])
        in_maps.append(m)
    res = run_bass_kernel_spmd(nc, in_maps, core_ids=list(range(ncores)))
    R = res.results
    cat0 = lambda k: np.concatenate([r[k] for r in R], axis=0)
    cat1 = lambda k: np.concatenate([r[k] for r in R], axis=1)
    nb = ncores * NSEQ; nsb = ncores * NS
    return (cat0("y_p"), cat0("y_s"),
            cat1("p_k").reshape(2, nb, 128, 4, 64), cat1("p_v").reshape(2, nb, 128, 4, 64),
            cat1("p_C"), cat1("p_n"), cat1("p_m"),
            cat1("s_k").reshape(2, nsb, 128, 4, 64), cat1("s_v").reshape(2, nsb, 128, 4, 64),
            cat1("s_C"), cat1("s_n"), cat1("s_m"))


def kernel(**inputs):
    outs = run_cores(full_cfg(), 8, inputs)
    return tuple(np.ascontiguousarray(o, dtype=np.float32) for o in outs)
```

```python
import math
from contextlib import ExitStack

import numpy as np
import concourse.bass as bass
import concourse.mybir as mybir
from concourse.bass_utils import run_bass_kernel_spmd

F32 = mybir.dt.float32
BF16 = mybir.dt.bfloat16
AF = mybir.ActivationFunctionType
ALU = mybir.AluOpType
AX = mybir.AxisListType

ENGS = ("pe", "act", "dve", "pool", "sp")
EPS = 1e-6
DIN = 4616
NEG = -30000.0


class Prog:
    def __init__(self, nc, es, same_engine_sync=True):
        self.nc = nc
        self.es = es
        self.q = {e: [] for e in ENGS}
        self.sems = {}
        self.count = {}
        self.waited = {e: {} for e in ENGS}
        self.lastw = {}
        self.readers = {}
        self.same_engine_sync = same_engine_sync
        self.nops = 0
        self.nwaits = 0
        import os
        self.max_ops = int(os.environ.get("KMAX", "0"))
        self.marks = []
        self.pe_log = []
        self.phase = "const"

    def sem(self, key):
        if key not in self.sems:
            self.sems[key] = self.es.enter_context(self.nc.semaphore("s_%s" % (key,)))
            self.count[key] = 0
        return self.sems[key]

    def op(self, eng, fn, reads=(), writes=(), semkey=None, inc=1):
        own = "E_" + eng
        if semkey is None:
            semkey = own
        if self.max_ops and self.nops >= self.max_ops:
            return None
        self.sem(semkey)
        deps = {}

        def add(d, kind):
            k, v = d
            if k == own:
                if eng == "pe" or not self.same_engine_sync:
                    return
            if deps.get(k, 0) < v:
                deps[k] = v

        for r in reads:
            if r in self.lastw:
                add(self.lastw[r], "raw")
        for w in writes:
            if w in self.lastw:
                add(self.lastw[w], "waw")
            for d in self.readers.get(w, ()):
                add(d, "war")
        for k, v in deps.items():
            if self.waited[eng].get(k, 0) < v:
                self.waited[eng][k] = v
                self.q[eng].append(("wait", self.sems[k], v))
                self.nwaits += 1
        self.count[semkey] += inc
        val = self.count[semkey]
        self.q[eng].append(("op", fn, self.sems[semkey], inc))
        self.nops += 1
        for w in writes:
            self.lastw[w] = (semkey, val)
            self.readers[w] = []
        for r in reads:
            self.readers.setdefault(r, []).append((semkey, val))
        return (semkey, val)

    def wait_all(self, eng, pred):
        for k, v in self.count.items():
            if not pred(k):
                continue
            if v > 0 and self.waited[eng].get(k, 0) < v:
                self.waited[eng][k] = v
                self.q[eng].append(("wait", self.sems[k], v))

    def emit(self):
        nc = self.nc
        with nc.Block() as block:
            def run(eng_name):
                def body(e):
                    for it in self.q[eng_name]:
                        if it[0] == "wait":
                            e.wait_ge(it[1], it[2])
                        else:
                            ins = it[1](e)
                            ins.then_inc(it[2], it[3])
                return body
            block.tensor(run("pe"))
            block.scalar(run("act"))
            block.vector(run("dve"))
            block.gpsimd(run("pool"))
            block.sync(run("sp"))


def t5_bucket(rel):
    half = 16
    exact = 8
    n = np.abs(rel)
    large = exact + (np.log(np.maximum(n, 1) / exact) / np.log(128 / exact) * (half - exact)).astype(np.int32)
    large = np.minimum(large, half - 1)
    return (rel > 0).astype(np.int32) * half + np.where(n < exact, n, large).astype(np.int32)


def make_onehot():
    oh = np.zeros((2, 33, 128, 128), np.float32)
    i = np.arange(128)[:, None]
    p = np.arange(128)[None, :]
    for tbl in range(2):
        rel = (p - 128 - i) if tbl == 0 else (p - i)
        bk = t5_bucket(rel)
        for b in range(32):
            oh[tbl, b] = (bk == b)
        if tbl == 0:
            oh[tbl, 32] = (p < 64) & (i >= 64)
        else:
            oh[tbl, 32] = (p >= 64) & (i < 64)
    return oh


def full_cfg():
    return dict(D=2048, DFF=8192, SEQ=2048, NSEQ=2, NS=2, LS=32, NW=2)


def build_program(cfg):
    D = cfg["D"]; DFF = cfg["DFF"]; SEQ = cfg["SEQ"]; NSEQ = cfg["NSEQ"]; NS = cfg["NS"]; LS = cfg["LS"]
    NW = cfg.get("NW", 2)
    KC = D // 128
    T = 512
    DCW = min(512, D); NDC = D // DCW
    HH = min(4096, DFF); NHALF = DFF // HH; NHC = HH // 128
    UPS = HH // 512
    DNK = min(16, NHC); DNS = NHC // DNK
    NPASS_SEQ = SEQ // T
    SCL = 128 ** -0.5

    nc = bass.Bass("TRN2", target_bir_lowering=False)

    def din(name, shape):
        return nc.dram_tensor(name, list(shape), F32, kind="ExternalInput").ap()

    def dout(name, shape):
        return nc.dram_tensor(name, list(shape), F32, kind="ExternalOutput").ap()

    xp = din("xp", [NSEQ, SEQ, D]); xsm = din("xs", [NS, LS, D])
    ck = din("ck", [2, NS, 128, 256]); cv = din("cv", [2, NS, 128, 256])
    sC = din("sC", [2, NS, 4, 256, 128]); sn = din("sn", [2, NS, 4, 128]); sm = din("sm", [2, NS, 4])
    rel_bias = din("rel_bias", [32, 16])
    g_mix = din("g_mix", [2, D]); w_in = din("w_in", [2, D, DIN])
    b_i = din("b_i", [2, 4]); b_f = din("b_f", [2, 4])
    g_q = din("g_q", [2, 64]); g_k = din("g_k", [2, 64]); sinks = din("sinks", [2, 16])
    g_mo = din("g_mo", [2, 1024]); g_ao = din("g_ao", [2, 1024])
    w_out = din("w_out", [2, 2048, D]); g_ffn = din("g_ffn", [2, D])
    w_up = din("w_up", [2, D, DFF]); w_down = din("w_down", [2, DFF, D])
    oh = din("oh", [2, 33, 128, 128])

    y_p = dout("y_p", [NSEQ, SEQ, D]); y_s = dout("y_s", [NS, LS, D])
    p_k = dout("p_k", [2, NSEQ, 128, 256]); p_v = dout("p_v", [2, NSEQ, 128, 256])
    p_C = dout("p_C", [2, NSEQ, 4, 256, 128]); p_n = dout("p_n", [2, NSEQ, 4, 128]); p_m = dout("p_m", [2, NSEQ, 4])
    s_k = dout("s_k", [2, NS, 128, 256]); s_v = dout("s_v", [2, NS, 128, 256])
    s_C = dout("s_C", [2, NS, 4, 256, 128]); s_n = dout("s_n", [2, NS, 4, 128]); s_m = dout("s_m", [2, NS, 4])

    es = ExitStack()
    P = Prog(nc, es)

    def sb(name, shape, dt):
        return es.enter_context(nc.sbuf_tensor(name, list(shape), dt))

    def ACT(out, in_, func, reads, writes, bias=None, scale=None, accum=None):
        kw_ = {}
        if bias is not None:
            kw_["bias"] = bias
        if scale is not None:
            kw_["scale"] = scale
        if accum is not None:
            kw_["accum_out"] = accum
        P.op("act", lambda e: e.activation(out=out, in_=in_, func=func, **kw_), reads=reads, writes=writes)

    def TT(eng, out, in0, in1, op, reads, writes):
        P.op(eng, lambda e: e.tensor_tensor(out=out, in0=in0, in1=in1, op=op), reads=reads, writes=writes)

    def TS(eng, out, in0, s1, s2, op0, op1, reads, writes):
        if op1 is None:
            P.op(eng, lambda e: e.tensor_scalar(out=out, in0=in0, scalar1=s1, scalar2=None, op0=op0),
                 reads=reads, writes=writes)
        else:
            P.op(eng, lambda e: e.tensor_scalar(out=out, in0=in0, scalar1=s1, scalar2=s2, op0=op0, op1=op1),
                 reads=reads, writes=writes)

    def STT(out, in0, scalar, in1, op0, op1, reads, writes):
        P.op("dve", lambda e: e.scalar_tensor_tensor(out=out, in0=in0, scalar=scalar, in1=in1, op0=op0, op1=op1),
             reads=reads, writes=writes)

    def TC(eng, out, in_, reads, writes):
        if eng == "act":
            ACT(out, in_, AF.Copy, reads, writes)
        else:
            P.op(eng, lambda e: e.tensor_copy(out=out, in_=in_), reads=reads, writes=writes)

    def RECIP(out, in_, reads, writes):
        P.op("dve", lambda e: e.reciprocal(out=out, in_=in_), reads=reads, writes=writes)

    def RECIP_LP(out, in_, reads, writes):
        def fn(e):
            with nc.allow_low_precision(reason="gate value is stored bf16 (it only multiplies a bf16 matmul operand)"):
                return e.reciprocal(out=out, in_=in_)
        P.op("dve", fn, reads=reads, writes=writes)

    def SCAN(out, d0, d1, init, op0, op1, reads, writes):
        P.op("dve", lambda e: e.tensor_tensor_scan(out=out, data0=d0, data1=d1, initial=init, op0=op0, op1=op1),
             reads=reads, writes=writes)

    def REDUCE(out, in_, reads, writes):
        P.op("dve", lambda e: e.tensor_reduce(out=out, in_=in_, axis=AX.X, op=ALU.add), reads=reads, writes=writes)

    def MEMSET(eng, ap, val, writes):
        P.op(eng, lambda e: e.memset(ap, val), writes=writes)

    def ASEL(out, in_, pattern, cmp, base, cm, reads, writes):
        P.op("pool", lambda e: e.affine_select(out=out, in_=in_, pattern=pattern, compare_op=cmp, fill=0.0,
                                               base=base, channel_multiplier=cm), reads=reads, writes=writes)

    def DMA(q, out, in_, reads, writes, semkey, slow=False):
        if slow:
            P.op(q, lambda e: e.dma_start(out=out, in_=in_, allow_slow_non_contiguous=True),
                 reads=reads, writes=writes, semkey=semkey, inc=16)
        else:
            P.op(q, lambda e: e.dma_start(out=out, in_=in_), reads=reads, writes=writes, semkey=semkey, inc=16)

    def MM(mms, reads, writes):
        mms = list(mms)
        P.pe_log.append((P.phase, len(mms)))

        def fn(e):
            last = None
            for (o, l_, r_, s0, s1) in mms:
                last = e.matmul(o, lhsT=l_, rhs=r_, start=s0, stop=s1)
            return last
        P.op("pe", fn, reads=reads, writes=writes)

    def TR(trs, reads, writes):
        trs = list(trs)
        P.pe_log.append((P.phase, len(trs)))

        def fn(e):
            last = None
            for (o, i_, idn) in trs:
                last = e.transpose(out=o, in_=i_, identity=idn)
            return last
        P.op("pe", fn, reads=reads, writes=writes)

    cp_eng = {"i": 0}

    def alt():
        cp_eng["i"] += 1
        return "act" if cp_eng["i"] % 2 else "dve"

    slabs = {}

    def reg_weight(name, l, src2d, specs):
        tot = len(specs)
        mx = max(nk * ncw for (_, _, nk, _, ncw) in specs)
        scr = nc.dram_tensor("scr_%s_%d" % (name, l), [tot, 128, mx], BF16, kind="Internal").ap()
        for i, (key, row0, nk, col0, ncw) in enumerate(specs):
            src = src2d[row0:row0 + nk * 128, col0:col0 + ncw].rearrange("(k p) c -> p k c", p=128)
            slabs[key] = dict(scr=scr[i, :, 0:nk * ncw], nk=nk, ncw=ncw, src=src, cast=False)

    IN_COLS = [(0, "big", 0), (512, "big", 1), (1024, "big", 2), (1536, "big", 3), (2048, "big", 4),
               (2560, "big", 5), (3080, "big", 6), (3592, "big", 7), (4104, "kvt", None), (3072, "gts", None)]
    for l in range(2):
        reg_weight("win", l, w_in[l], [(("win", l, j), 0, KC, c0, (8 if kind == "gts" else 512))
                                       for j, (c0, kind, _) in enumerate(IN_COLS)])
        reg_weight("wout", l, w_out[l], [(("wout", l, cg), 0, 16, cg * DCW, DCW) for cg in range(NDC)])
        reg_weight("wup", l, w_up[l], [(("wup", l, hf, j), 0, KC, hf * HH + j * 512, 512)
                                       for hf in range(NHALF) for j in range(UPS)])
        reg_weight("wdn", l, w_down[l], [(("wdn", l, hf, cg, s), hf * HH + s * DNK * 128, DNK, cg * DCW, DCW)
                                         for hf in range(NHALF) for cg in range(NDC) for s in range(DNS)])

    passes = []
    for si in range(NSEQ):
        for ti in range(NPASS_SEQ):
            passes.append(("p", si, ti))
    if NS > 0:
        passes.append(("s", 0, 0))
    sched = []
    for ps_ in passes:
        for l in range(2):
            for j in range(len(IN_COLS)):
                sched.append(("win", l, j))
            for cg in range(NDC):
                sched.append(("wout", l, cg))
            for hf in range(NHALF):
                for j in range(UPS):
                    sched.append(("wup", l, hf, j))
                for cg in range(NDC):
                    for s in range(DNS):
                        sched.append(("wdn", l, hf, cg, s))
    wbuf = [sb("wbuf%d" % s, [128, 16 * 512], BF16) for s in range(NW)]
    wslots = [w_[:] for w_ in wbuf]
    per_pass = len(sched) // len(passes)
    extra_slot = (NS > 0) and (4 * D >= 16 * 512) and NW == 2
    j0s = len(sched) - per_pass if NS > 0 else len(sched)

    def slot_of(j):
        if extra_slot and j >= j0s:
            return [j0s % 2, (j0s + 1) % 2, 2][(j - j0s) % 3]
        return j % NW

    def nslots(j):
        return 3 if (extra_slot and j >= j0s) else NW
    wstate = {"loaded": 0, "used": 0, "cast": 0, "ncast": 0}
    CL = 26
    NCS = 32

    def cast_upto(j):
        while wstate["cast"] <= min(j, len(sched) - 1):
            key = sched[wstate["cast"]]
            wstate["cast"] += 1
            sl = slabs[key]
            if sl["cast"]:
                continue
            sl["cast"] = True
            ck_ = "cast%d" % (wstate["ncast"] % NCS)
            wstate["ncast"] += 1
            DMA("pool", sl["scr"].rearrange("p (k c) -> p k c", k=sl["nk"]), sl["src"], [], [("scr", key)], ck_)

    def load_next():
        j = wstate["loaded"]
        key = sched[j]
        sl = slabs[key]
        s = slot_of(j)
        wr = [("w", s)] + ([("x", 2), ("x", 3)] if s == 2 else [])
        DMA("sp", wslots[s][:, 0:sl["nk"] * sl["ncw"]], sl["scr"], [("scr", key)], wr, "wl%d" % s)
        wstate["loaded"] += 1

    def prefetch_extra():
        j = wstate["used"]
        if j >= len(sched):
            return
        cast_upto(j + nslots(j) + CL)
        while wstate["loaded"] < min(len(sched), j + nslots(j)):
            load_next()

    def next_slab(key):
        j = wstate["used"]
        assert sched[j] == key, (sched[j], key)
        cast_upto(j + NW + CL)
        while wstate["loaded"] < min(len(sched), j + nslots(j)):
            load_next()
        wstate["used"] += 1
        sl = slabs[key]
        s = slot_of(j)
        return wslots[s][:, 0:sl["nk"] * sl["ncw"]].rearrange("p (k c) -> p k c", k=sl["nk"]), ("w", s)

    banks = [es.enter_context(nc.psum_tensor("ps%d" % b, [128, 512], F32)) for b in range(8)]
    from collections import deque
    pfree_list = deque(range(7))

    def psum():
        assert pfree_list, "out of PSUM banks"
        b = pfree_list.popleft()
        return banks[b], ("ps", b)

    def pfree(key):
        assert key[1] not in pfree_list
        pfree_list.append(key[1])
    pm = banks[7]
    pmk = ("ps", 7)

    x0_ = sb("x0", [128, D], F32)
    x1_ = sb("x1", [128, D], F32)
    x23 = sb("x23", [128, 2 * D], F32)
    x = [x0_[:], x1_[:], x23[:, 0:D], x23[:, D:2 * D]]
    if extra_slot:
        wslots.append(x23[:].bitcast(BF16)[:, 0:16 * 512])
    actT = sb("actT", [128, 16, 512], BF16)
    big = sb("big", [128, 16384], BF16)
    bigp = big[:].rearrange("p (t c) -> p t c", t=4)
    uT = big[:].rearrange("p (h n) -> p h n", n=512)
    kvt = [sb("kvt%d" % t, [128, 512], F32) for t in range(4)]
    gts = [sb("gts%d" % t, [128, 8], F32) for t in range(4)]
    BT = [sb("BT%d" % t, [128, 16, 128], F32) for t in range(2)]
    xs_t = [sb("xs%d" % i, [128, D], BF16) for i in range(2)]
    identb = sb("identb", [128, 128], BF16)
    identf = sb("identf", [128, 128], F32)
    ones4 = sb("ones4", [4, 128], F32)
    zer4 = sb("zer4", [4, 128], F32)
    selh = sb("selh", [4, 4, 128], F32)
    gcol = sb("gcol", [128, 2, 2, 16], F32)
    gmo_c = sb("gmo_c", [128, 2, 8], F32)
    gao_c = sb("gao_c", [128, 2, 8], F32)
    gq_c = sb("gq_c", [64, 2], F32)
    gk_c = sb("gk_c", [64, 2], F32)
    gq8_c = sb("gq8_c", [64, 2], F32)
    gk_bc = sb("gk_bc", [128, 2, 64], F32)
    bi_c = sb("bi_c", [4, 2], F32)
    nbf_c = sb("nbf_c", [4, 2], F32)
    sinkexp = sb("sinkexp", [128, 32], F32)
    rbx = sb("rbx", [33, 16], F32)
    st = sb("st", [128, 64], F32)
    Cst = [sb("C%d" % l, [128, 4, 257], F32) for l in range(2)]
    Cbf = [sb("Cbf%d" % l, [128, 4, 258], BF16) for l in range(2)]
    kTpp = [[sb("kTpp%d_%d" % (l, i), [64, 4, 128], BF16) for i in range(2)] for l in range(2)]
    vApp = [[sb("vApp%d_%d" % (l, i), [128, 4, 66], BF16) for i in range(2)] for l in range(2)]
    carL = [sb("carL%d" % l, [4, 1], F32) for l in range(2)]
    carM = [sb("carM%d" % l, [4, 1], F32) for l in range(2)]
    g_ig = sb("g_ig", [4, 128], F32); g_l = sb("g_l", [4, 128], F32); g_Lc = sb("g_Lc", [4, 128], F32)
    g_A = sb("g_A", [4, 128], F32); g_mu = sb("g_mu", [4, 128], F32); g_g = sb("g_g", [4, 128], F32)
    g_md = sb("g_md", [4, 128], F32); g_wk = sb("g_wk", [4, 128], F32)
    g_s = sb("g_s", [4, 8], F32)
    dd4 = sb("dd4", [4, 4], F32)
    gpt = sb("gpt", [128, 16], F32)
    decbc = sb("decbc", [128, 4], F32)
    qT = sb("qT", [128, 4, 128], BF16); kT = sb("kT", [128, 4, 128], BF16)
    vext = sb("vext", [128, 4, 258], BF16)
    ET = [sb("ET%d" % i, [128, 128], F32) for i in range(2)]
    maskB = sb("maskB", [128, 128], F32)
    wT = [sb("wT_%d" % i, [128, 128], BF16) for i in range(2)]
    kw = [sb("kw%d" % i, [128, 128], BF16) for i in range(2)]
    inter_s0 = sb("inter_s0", [128, 257], F32)
    TA = sb("TA", [128, 4, 257], F32)
    TAf = TA[:].rearrange("p h c -> p (h c)")
    TB1 = sb("TB1", [128, 1024], BF16)
    TB2 = sb("TB2", [128, 1024], BF16)
    TA2 = sb("TA2", [128, 1024], F32)
    TA2f = TA2[:]
    TB1b = sb("TB1b", [128, 1024], BF16)
    inter_s = [inter_s0[:], TB2[:].bitcast(F32)[:, 0:257]]
    inter_k = [("inter_s", 0), "TB2"]
    TB2b = sb("TB2b", [128, 1024], BF16)
    ks = sb("ks", [128, 256], BF16)
    knf = sb("knf", [128, 256], F32)
    qnT = sb("qnT", [64, 16, 128], BF16)
    sbS = [sb("sbS%d" % i, [128, 512], F32) for i in range(2)]
    PT = [sb("PT%d" % i, [128, 512], BF16) for i in range(2)]

    CH = lambda l: [("C", l, h) for h in range(4)]
    CBH = lambda l: [("Cbf", l, h) for h in range(4)]

    MEMSET("pool", identf[:], 1.0, ["identf"])
    ASEL(identf[:], identf[:], [[-1, 128]], ALU.is_equal, 0, 1, ["identf"], ["identf"])
    TC("pool", identb[:], identf[:], ["identf"], ["identb"])
    MEMSET("pool", ones4[:], 1.0, ["ones4"])
    MEMSET("pool", maskB[:], 30000.0, ["maskB"])
    ASEL(maskB[:], maskB[:], [[-1, 128]], ALU.is_gt, 0, 1, ["maskB"], ["maskB"])
    MEMSET("pool", zer4[:], 0.0, ["zer4"])
    MEMSET("pool", selh[:], 1.0, ["selh"])
    for h in range(4):
        ASEL(selh[:, h, :], selh[:, h, :], [[0, 128]], ALU.is_equal, -h, 1, ["selh"], ["selh"])
    for l in range(2):
        for pi_ in range(2):
            MEMSET("pool", vApp[l][pi_][:, :, 64:65], 1.0, [("vA1", l, pi_)])
    MEMSET("pool", vext[:, :, 256:257], 1.0, ["vext1"])
    MEMSET("pool", rbx[:], NEG, ["rbx"])
    cq = "cst"
    cres = ["gcol", "gmo_c", "gao_c", "gq_c", "gk_c", "gk_bc", "bi_c", "nbf_c", "sinkexp", "rbx"]
    for l in range(2):
        DMA("sp", gcol[:, l, 0, 0:KC], g_mix[l].rearrange("(k p) -> p k", p=128), [], [("cst_tmp", "gcol_mix", l)], cq, slow=True)
        DMA("sp", gcol[:, l, 1, 0:KC], g_ffn[l].rearrange("(k p) -> p k", p=128), [], [("cst_tmp", "gcol", l)], cq, slow=True)
        DMA("sp", gmo_c[:, l, :], g_mo[l].rearrange("(k p) -> p k", p=128), [], [("cst_tmp", "gmo_c", l)], cq, slow=True)
        DMA("sp", gao_c[:, l, :], g_ao[l].rearrange("(k p) -> p k", p=128), [], [("cst_tmp", "gao_c", l)], cq, slow=True)
        DMA("sp", gq_c[:, l:l + 1], g_q[l].rearrange("(p o) -> p o", o=1), [], [("cst_tmp", "gq_c", l)], cq, slow=True)
        DMA("sp", gk_c[:, l:l + 1], g_k[l].rearrange("(p o) -> p o", o=1), [], [("cst_tmp", "gk_c", l)], cq, slow=True)
        DMA("sp", gk_bc[:, l, :], g_k[l:l + 1, :].broadcast_to([128, 64]), [], [("cst_tmp", "gk_bc", l)], cq, slow=True)
        DMA("sp", bi_c[:, l:l + 1], b_i[l].rearrange("(p o) -> p o", o=1), [], [("cst_tmp", "bi_c", l)], cq, slow=True)
        DMA("sp", nbf_c[:, l:l + 1], b_f[l].rearrange("(p o) -> p o", o=1), [], [("cst_tmp", "nbf_c", l)], cq, slow=True)
        DMA("sp", sinkexp[:, l * 16:(l + 1) * 16], sinks[l:l + 1, :].broadcast_to([128, 16]), [], [("cst_tmp", "sinkexp", l)], cq,
            slow=True)
    DMA("sp", rbx[0:32, :], rel_bias, ["rbx"], [("cst_tmp", "rbx", 0)], cq)
    for r in cres:
        P.lastw[r] = (cq, P.count[cq])
    ACT(sinkexp[:], sinkexp[:], AF.Exp, ["sinkexp"], ["sinkexp"])
    TS("dve", gq8_c[:], gq_c[:], 0.125, None, ALU.mult, None, ["gq_c"], ["gq8_c"])
    TS("dve", nbf_c[:], nbf_c[:], -1.0, None, ALU.mult, None, ["nbf_c"], ["nbf_c"])
    for tbl in range(2):
        for i0 in range(0, 128, 32):
            pb, pbk = psum()
            for sub in range(8):
                bi_ = sub % 2
                buf = sbS[bi_][0:33, :].rearrange("p (a k) -> p a k", a=4)
                bk = ("sbS", bi_)
                DMA("sp", buf, oh[tbl, :, i0 + sub * 4:i0 + sub * 4 + 4, :], [], [bk], "ohl%d" % bi_)
                MM([(pb[:, (sub * 4 + ii) * 16:(sub * 4 + ii + 1) * 16], buf[:, ii, :], rbx[:, :], True, True)
                    for ii in range(4)], [bk, "rbx"], [pbk])
            TC("dve", BT[tbl][:, :, i0:i0 + 32].rearrange("p h i -> p i h"),
               pb[:, :].rearrange("p (i h) -> p i h", h=16), [pbk], [("BT", tbl)])
            pfree(pbk)

    def rmsnorm_all(tiles, l, which):
        ntt = len(tiles)
        n = tiles[0]
        for tt in range(ntt):
            xs = xs_t[tt % 2]
            ACT(xs[:n, :], x[tt][:n, :], AF.Square, [("x", tt)], ["st_n0", ("xs", tt % 2)], accum=st[:n, tt:tt + 1])
        ACT(st[:n, 0:ntt], st[:n, 0:ntt], AF.Ln, ["st_n0"], ["st_n0"], scale=1.0 / D, bias=EPS)
        ACT(st[:n, 4:4 + ntt], st[:n, 0:ntt], AF.Exp, ["st_n0"], ["st_n3"], scale=-0.5)
        for tt in range(ntt):
            xs = xs_t[tt % 2]
            xk = ("xs", tt % 2)
            ACT(xs[:n, :], x[tt][:n, :], AF.Copy, [("x", tt), "st_n3"], [xk], scale=st[:n, 4 + tt:5 + tt])
            for f0 in range(0, KC, 4):
                nf = min(4, KC - f0)
                pb, pbk = psum()
                pbb = pb[:].bitcast(BF16)
                TR([(pbb[:, j * 128:j * 128 + n], xs[:n, (f0 + j) * 128:(f0 + j + 1) * 128], identb[:n, :n])
                    for j in range(nf)], [xk, "identb"], [pbk])
                TT("dve", actT[:, f0:f0 + nf, tt * 128:tt * 128 + n],
                   pbb[:, 0:nf * 128].rearrange("p (k c) -> p k c", k=nf)[:, :, 0:n],
                   gcol[:, l, which, f0:f0 + nf].unsqueeze(2).broadcast_to([128, nf, n]), ALU.mult,
                   [pbk, "gcol"], [("aT", tt, 0), ("aT", tt, 1)])
                pfree(pbk)

    def prompt_state_init(l):
        MEMSET("pool", Cst[l][:], 0.0, CH(l))
        MEMSET("pool", Cbf[l][:], 0.0, CBH(l))
        MEMSET("pool", carL[l][:], 0.0, [("carL", l)])
        MEMSET("pool", carM[l][:], 0.0, [("carM", l)])

    TAH = [("TA", h) for h in range(4)]
    TA2H = [("TA2", h) for h in range(4)]

    def sample_mlstm_load(l, j):
        stg = TAf[:, 0:1024].rearrange("p (a k) -> p a k", a=8)
        DMA("pool", stg, sC[l, j].rearrange("h (c p) k -> p (h c) k", p=128), [], TAH, "sldC")
        yield
        for h0 in range(0, 4, 2):
            pb, pbk = psum()
            TR([(pb[:, (hh * 2 + c2) * 128:(hh * 2 + c2 + 1) * 128], stg[:, (h0 + hh) * 2 + c2, :], identf[:])
                for hh in range(2) for c2 in range(2)], TAH + ["identf"], [pbk])
            TC("dve", Cst[l][:, h0:h0 + 2, 0:256], pb[:, :].rearrange("p (h v) -> p h v", h=2), [pbk], CH(l))
            pfree(pbk)
            yield
        DMA("pool", Cst[l][:, :, 256:257].rearrange("p h o -> p (h o)"), sn[l, j].rearrange("h k -> k h"),
            CH(l), CH(l), "sldn", slow=True)
        ACT(Cbf[l][:, :, 0:257], Cst[l][:], AF.Copy, CH(l), CBH(l))
        MEMSET("pool", carL[l][:], 0.0, [("carL", l)])
        DMA("pool", carM[l][:], sm[l, j].rearrange("(p o) -> p o", o=1), [], [("carM", l)], "sldm", slow=True)
        yield

    def sample_attn_load(l, j):
        stk = TA2f[:, 0:256]
        DMA("pool", stk, ck[l, j], TA2H, TA2H, "sldk")
        TC("dve", ks[:, :], stk, TA2H, ["ks"])
        yield
        pb, pbk = psum()
        pbb = pb[:].bitcast(BF16)
        TR([(pbb[0:64, kh * 128:(kh + 1) * 128], ks[:, kh * 64:(kh + 1) * 64], identb[:]) for kh in range(4)],
           ["ks", "identb"], [pbk])
        ACT(kTpp[l][0][:], pbb[0:64, 0:512].rearrange("p (h n) -> p h n", h=4), AF.Copy, [pbk], [("kTpp", l, 0)])
        pfree(pbk)
        yield
        stv = TA2f[:, 256:512]
        DMA("pool", stv, cv[l, j], TA2H, TA2H, "sldv")
        TC("dve", vApp[l][0][:, :, 0:64], stv.rearrange("p (h d) -> p h d", h=4), TA2H + [("vA1", l, 0)], [("vA", l, 0)])
        DMA("pool", s_k[l, j, 0:128 - LS, :], ck[l, j, LS:128, :], [], [], "so_cp")
        DMA("pool", s_v[l, j, 0:128 - LS, :], cv[l, j, LS:128, :], [], [], "so_cp")
        yield

    def mlstm_gates(l, tt, n, fin):
        cL = ("carL", l); cM = ("carM", l)
        pg, pgk = psum()
        TR([(pg[0:4, 0:n], gts[tt][:n, 0:4], identf[:n, :n]),
            (pg[0:4, 128:128 + n], gts[tt][:n, 4:8], identf[:n, :n])], [("gts", tt), "identf"], [pgk])
        ACT(g_ig[:, :n], pg[0:4, 0:n], AF.Identity, [pgk, "bi_c"], ["g_ig"], bias=bi_c[:, l:l + 1])
        ACT(g_l[:, :n], pg[0:4, 128:128 + n], AF.Exp, [pgk, "nbf_c"], ["g_l"], scale=-1.0, bias=nbf_c[:, l:l + 1])
        pfree(pgk)
        yield
        ACT(g_l[:, :n], g_l[:, :n], AF.Ln, ["g_l"], ["g_l"], bias=1.0)
        SCAN(g_Lc[:, :n], g_l[:, :n], zer4[:, :n], carL[l][:, 0:1], ALU.add, ALU.add, ["g_l", "zer4", cL], ["g_Lc"])
        yield
        TT("dve", g_A[:, :n], g_ig[:, :n], g_Lc[:, :n], ALU.add, ["g_ig", "g_Lc"], ["g_A"])
        SCAN(g_mu[:, :n], g_A[:, :n], zer4[:, :n], carM[l][:, 0:1], ALU.max, ALU.add, ["g_A", "zer4", cM], ["g_mu"])
        yield
        ACT(g_g[:, :n], g_mu[:, :n], AF.Exp, ["g_mu", cM], ["g_g"], scale=-1.0, bias=carM[l][:, 0:1])
        TT("dve", g_md[:, :n], g_Lc[:, :n], g_mu[:, :n], ALU.subtract, ["g_Lc", "g_mu"], ["g_md"])
        yield
        ACT(g_md[:, :n], g_md[:, :n], AF.Exp, ["g_md"], ["g_md"])
        TS("dve", g_s[:, 0:1], g_mu[:, n - 1:n], -1.0, None, ALU.mult, None, ["g_mu"], ["g_s0"])
        yield
        ACT(g_wk[:, :n], g_A[:, :n], AF.Exp, ["g_A", "g_s0"], ["g_wk"], bias=g_s[:, 0:1])
        ACT(g_s[:, 1:2], carM[l][:, 0:1], AF.Exp, [cM, "g_s0"], ["g_s1"], bias=g_s[:, 0:1])
        if fin:
            TT("dve", g_s[:, 2:3], g_mu[:, n - 1:n], g_Lc[:, n - 1:n], ALU.subtract, ["g_mu", "g_Lc"], ["g_s2"])
        TC("dve", carL[l][:, 0:1], g_Lc[:, n - 1:n], ["g_Lc"], [cL])
        TC("dve", carM[l][:, 0:1], g_mu[:, n - 1:n], ["g_mu"], [cM])
        yield

    def mlstm_tile(l, tt, n, fin, kind, si, side=None):
        mqv = bigp[:, tt, 0:512]; mkv = bigp[:, tt, 512:1024]
        mvv = bigp[:, tt, 1024:2048]; mov = bigp[:, tt, 2048:3072]
        rbig = [("big", tt * 8 + i) for i in range(6)]
        sd = [side]

        def tick():
            if sd[0] is not None:
                try:
                    next(sd[0])
                except StopIteration:
                    sd[0] = None
        pt, ptk = psum()
        TR([(pt[:n, 0:4], g_A[:, :n], identf[0:4, 0:4]), (pt[:n, 4:8], g_g[:, :n], identf[0:4, 0:4]),
            (pt[:n, 8:12], g_md[:, :n], identf[0:4, 0:4]), (pt[:n, 12:16], g_wk[:, :n], identf[0:4, 0:4])],
           ["g_A", "g_g", "g_md", "g_wk", "identf"], [ptk])
        TC("dve", gpt[:n, :], pt[:n, 0:16], [ptk], ["gpt"])
        pfree(ptk)
        TS("dve", dd4[:], identf[0:4, 0:4], g_s[:, 1:2], None, ALU.mult, None, ["identf", "g_s1"], ["dd4"])
        yield
        pd, pdk = psum()
        MM([(pd[:, 0:4], ones4[:, :], dd4[:, :], True, True)], ["ones4", "dd4"], [pdk])
        TC("dve", decbc[:], pd[:, 0:4], [pdk], ["decbc"])
        pfree(pdk)
        mm_ = []
        for h in range(4):
            mm_.append((pm[:, h * 128:h * 128 + n], selh[:, h, :], g_mu[:, :n], True, False))
            mm_.append((pm[:n, h * 128:h * 128 + n], identf[:n, :n], maskB[:n, :n], False, True))
        MM(mm_, ["selh", "g_mu", "identf", "maskB"], [pmk])
        yield
        pq, pqk = psum()
        pqb = pq[:].bitcast(BF16)
        TR([(pqb[:, h * 128:h * 128 + n], mqv[:n, h * 128:(h + 1) * 128], identb[:n, :n]) for h in range(4)] +
           [(pqb[:, 512 + h * 128:512 + h * 128 + n], mkv[:n, h * 128:(h + 1) * 128], identb[:n, :n])
            for h in range(4)], rbig[0:2] + ["identb"], [pqk])
        ACT(qT[:, :, :n], pqb[:, 0:512].rearrange("p (h c) -> p h c", h=4)[:, :, :n], AF.Copy, [pqk], ["qT"])
        ACT(kT[:, :, :n], pqb[:, 512:1024].rearrange("p (h c) -> p h c", h=4)[:, :, :n], AF.Copy, [pqk], ["kT"])
        pfree(pqk)
        tick()
        yield
        TC("act", vext[:n, :, 0:256], mvv[:n, :].rearrange("p (h v) -> p h v", h=4), rbig[2:4] + ["vext1"], ["vext"])
        ACT(TB1[:n, :], mov[:n, :], AF.Exp, rbig[4:6], ["TB1"], scale=-1.0)
        TS("dve", TB1[:n, :], TB1[:n, :], 1.0, None, ALU.add, None, ["TB1"], ["TB1"])
        RECIP_LP(TB1[:n, :], TB1[:n, :], ["TB1"], ["TB1"])
        tick()
        yield
        def head_steps(h):
            i2 = h % 2
            pS, pSk = psum()
            MM([(pS[:n, 0:n], kT[:, h, :n], qT[:, h, :n], True, True)], ["kT", "qT"], [pSk])
            ACT(ET[i2][:n, :n], pm[:n, h * 128:h * 128 + n], AF.Exp, [pmk, "gpt"], [("ET", i2)], scale=-1.0,
                bias=gpt[:n, h:h + 1])
            yield
            STT(wT[i2][:n, :n], pS[:n, 0:n], SCL, ET[i2][:n, :n], ALU.mult, ALU.mult, [pSk, ("ET", i2)], [("wT", i2)])
            pfree(pSk)
            TS("dve", kw[i2][:n, :], mkv[:n, h * 128:(h + 1) * 128], gpt[:n, 12 + h:13 + h], SCL, ALU.mult, ALU.mult,
               [rbig[1], "gpt"], [("kw", i2)])
            yield
            pJ, pJk = psum()
            MM([(pJ[:n, 0:257], qT[:, h, :n], Cbf[l][:, h, 0:257], True, True)], ["qT", ("Cbf", l, h)], [pJk])
            ACT(inter_s[i2][:n, :], pJ[:n, 0:257], AF.Copy, [pJk, "gpt"], [inter_k[i2]], scale=gpt[:n, 4 + h:5 + h])
            pfree(pJk)
            yield
            pI, pIk = psum()
            MM([(pI[:n, 0:257], wT[i2][:n, :n], vext[:n, h, 0:257], True, True)], [("wT", i2), "vext"], [pIk])
            TT("dve", TA[:n, h, :], pI[:n, 0:257], inter_s[i2][:n, :], ALU.add, [pIk, inter_k[i2]], [("TA", h)])
            pfree(pIk)
            yield
            pC, pCk = psum()
            MM([(pC[:, 0:257], kw[i2][:n, :], vext[:n, h, 0:257], True, True)], [("kw", i2), "vext"], [pCk])
            STT(Cst[l][:, h, :], Cst[l][:, h, :], decbc[:, h:h + 1], pC[:, 0:257], ALU.mult, ALU.add,
                [pCk, "decbc", ("C", l, h)], [("C", l, h)])
            pfree(pCk)
            TC("pool", Cbf[l][:, h, 0:257], Cst[l][:, h, :], [("C", l, h)], [("Cbf", l, h)])
            yield

        for hp in ((0, 1), (2, 3)):
            ga, gb = head_steps(hp[0]), head_steps(hp[1])
            alive = [ga, gb]
            while alive:
                for g_ in list(alive):
                    try:
                        next(g_)
                    except StopIteration:
                        alive.remove(g_)
                tick()
                yield
        hn_den = TA[:n, :, 256:257].rearrange("p h o -> p (h o)")
        ACT(st[:n, 8:12], hn_den, AF.Abs, TAH, ["st_m0"])
        TT("dve", st[:n, 8:12], st[:n, 8:12], gpt[:n, 8:12], ALU.max, ["st_m0", "gpt"], ["st_m0"])
        tick()
        yield
        RECIP(st[:n, 12:16], st[:n, 8:12], ["st_m0"], ["st_m1"])
        for h in range(4):
            ACT(TB2[:n, h * 256:(h + 1) * 256], TA[:n, h, 0:256], AF.Square, [("TA", h)], [("st_m2", h), "TB2"],
                accum=st[:n, 16 + h:17 + h])
            if h % 2:
                tick()
                yield
        TT("dve", st[:n, 20:24], st[:n, 12:16], st[:n, 12:16], ALU.mult, ["st_m1"], ["st_m3"])
        TT("dve", st[:n, 20:24], st[:n, 20:24], st[:n, 16:20], ALU.mult, ["st_m3"] + [("st_m2", h) for h in range(4)],
           ["st_m3"])
        tick()
        yield
        ACT(st[:n, 24:28], st[:n, 20:24], AF.Ln, ["st_m3"], ["st_m4"], scale=1.0 / 256, bias=EPS)
        ACT(st[:n, 28:32], st[:n, 24:28], AF.Exp, ["st_m4"], ["st_m5"], scale=-0.5)
        tick()
        yield
        TT("dve", st[:n, 28:32], st[:n, 28:32], st[:n, 12:16], ALU.mult, ["st_m5", "st_m1"], ["st_m5"])
        tick()
        yield
        for h in range(4):
            STT(TB2[:n, h * 256:(h + 1) * 256], TA[:n, h, 0:256], st[:n, 28 + h:29 + h], TB1[:n, h * 256:(h + 1) * 256],
                ALU.mult, ALU.mult, [("TA", h), "st_m5", "TB1"], ["TB2"])
            if h % 2:
                tick()
                yield
        ph, phk = psum()
        phb = ph[:].bitcast(BF16)
        TR([(phb[:, fc * 128:fc * 128 + n], TB2[:n, fc * 128:(fc + 1) * 128], identb[:n, :n]) for fc in range(8)],
           ["TB2", "identb"], [phk])
        TT("dve", actT[:, 0:8, tt * 128:tt * 128 + n], phb[:, 0:1024].rearrange("p (k c) -> p k c", k=8)[:, :, 0:n],
           gmo_c[:, l, :].unsqueeze(2).broadcast_to([128, 8, n]), ALU.mult, [phk, "gmo_c"], [("aT", tt, 0)])
        pfree(phk)
        tick()
        yield
        if fin:
            b = si if kind == "p" else tt
            oC, on_, om = (p_C, p_n, p_m) if kind == "p" else (s_C, s_n, s_m)
            sk_ = "so_m_%s%d_%d" % (kind, l, b)
            stg = TAf[:, 0:1024].rearrange("p (a k) -> p a k", a=8)
            for h0 in range(0, 4, 2):
                pb, pbk = psum()
                TR([(pb[:, (hh * 2 + c2) * 128:(hh * 2 + c2 + 1) * 128], Cst[l][:, h0 + hh, c2 * 128:(c2 + 1) * 128],
                     identf[:]) for hh in range(2) for c2 in range(2)], CH(l) + ["identf"], [pbk])
                TC("dve", stg[:, h0 * 2:h0 * 2 + 4, :], pb[:, :].rearrange("p (a k) -> p a k", a=4), [pbk], TAH)
                pfree(pbk)
                tick()
                yield
            DMA("pool", oC[l, b].rearrange("h (c p) k -> p (h c) k", p=128), stg, TAH, [], sk_)
            DMA("pool", on_[l, b].rearrange("h k -> k h"), Cst[l][:, :, 256:257].rearrange("p h o -> p (h o)"),
                CH(l), [], sk_, slow=True)
            DMA("pool", om[l, b].rearrange("(p o) -> p o", o=1), g_s[:, 2:3], ["g_s2"], [], sk_, slow=True)
            tot = (sk_, P.count.get(sk_, 0))
            for r in TAH + CH(l) + ["g_s2"]:
                P.readers.setdefault(r, []).append(tot)
            tick()
            yield
        while sd[0] is not None:
            tick()
            yield

    def attn_tile(l, tt, n, has_prev, par, fin, kind, si):
        aqv = bigp[:, tt, 3072:4096]
        raq = [("big", tt * 8 + 6), ("big", tt * 8 + 7)]
        kcur = kTpp[l][par]; kprev = kTpp[l][1 - par]
        vcur = vApp[l][par]; vprev = vApp[l][1 - par]
        kck = ("kTpp", l, par); kpk = ("kTpp", l, 1 - par)
        vck = ("vA", l, par); vpk = ("vA", l, 1 - par)
        npv = 128
        g4 = lambda ap2, m: ap2.rearrange("p (g c) -> p g c", g=m)
        ACT(TA2f[:n, 0:1024], aqv[:n, :], AF.Square, raq, TA2H)
        REDUCE(st[:n, 32:48], g4(TA2f[:n, 0:1024], 16), TA2H, ["st_a0"])
        yield
        ACT(TA2f[:n, 0:256], kvt[tt][:n, 0:256], AF.Square, [("kvt", tt)], TA2H)
        REDUCE(st[:n, 48:52], g4(TA2f[:n, 0:256], 4), TA2H, ["st_a1"])
        yield
        ACT(st[:n, 32:52], st[:n, 32:52], AF.Ln, ["st_a0", "st_a1"], ["st_a2"], scale=1.0 / 64, bias=EPS)
        ACT(st[:n, 32:52], st[:n, 32:52], AF.Exp, ["st_a2"], ["st_a2"], scale=-0.5)
        yield
        TT("dve", g4(TB1b[:n, :], 16), g4(aqv[:n, :], 16), st[:n, 32:48].unsqueeze(2).broadcast_to([n, 16, 64]),
           ALU.mult, raq + ["st_a2"], ["TB1b"])
        yield
        TT("dve", g4(knf[:n, :], 4), g4(kvt[tt][:n, 0:256], 4), st[:n, 48:52].unsqueeze(2).broadcast_to([n, 4, 64]),
           ALU.mult, [("kvt", tt), "st_a2"], ["knf"])
        TC("dve", ks[:n, :], knf[:n, :], ["knf"], ["ks"])
        yield
        for h0 in (0, 8):
            pb, pbk = psum()
            pbb = pb[:].bitcast(BF16)
            TR([(pbb[0:64, j * 128:j * 128 + n], TB1b[:n, (h0 + j) * 64:(h0 + j + 1) * 64], identb[:n, :n])
                for j in range(8)], ["TB1b", "identb"], [pbk])
            ACT(qnT[:, h0:h0 + 8, :n], g4(pbb[0:64, 0:1024], 8)[:, :, :n], AF.Copy, [pbk, "gq8_c"], ["qnT"],
                scale=gq8_c[:, l:l + 1])
            pfree(pbk)
            yield
        pb, pbk = psum()
        pbb = pb[:].bitcast(BF16)
        TR([(pbb[0:64, j * 128:j * 128 + n], ks[:n, j * 64:(j + 1) * 64], identb[:n, :n]) for j in range(4)],
           ["ks", "identb"], [pbk])
        ACT(kcur[:, :, :n], g4(pbb[0:64, 0:512], 4)[:, :, :n], AF.Copy, [pbk, "gk_c"], [kck], scale=gk_c[:, l:l + 1])
        pfree(pbk)
        TC("pool", vcur[:n, :, 0:64], g4(kvt[tt][:n, 256:512], 4), [("kvt", tt), ("vA1", l, par)], [vck])
        yield
        if fin:
            b = si if kind == "p" else tt
            ok_, ov_ = (p_k, p_v) if kind == "p" else (s_k, s_v)
            r0 = 0 if kind == "p" else 128 - n
            sk_ = "so_a_%s%d_%d" % (kind, l, b)
            TT("dve", g4(knf[:n, :], 4), g4(knf[:n, :], 4), gk_bc[:n, l, :].unsqueeze(1).broadcast_to([n, 4, 64]),
               ALU.mult, ["knf", "gk_bc"], ["knf"])
            DMA("pool", ok_[l, b, r0:r0 + n, :], knf[:n, :], ["knf"], [], sk_)
            DMA("pool", ov_[l, b, r0:r0 + n, :], kvt[tt][:n, 256:512], [("kvt", tt)], [], sk_)
            tot = (sk_, P.count.get(sk_, 0))
            for r in ["knf", ("kvt", tt)]:
                P.readers.setdefault(r, []).append(tot)
            yield
        for kvh in range(4):
            hs = slice(kvh * 4, kvh * 4 + 4)
            if has_prev:
                pS, pSk = psum()
                MM([(g4(pS[:npv, 0:512], 4)[:, :, :n], kprev[:, kvh, :npv], qnT[:, hs, :n], True, True)],
                   [kpk, "qnT"], [pSk])
                TT("dve", g4(sbS[0][:npv, :], 4)[:, :, :n], g4(pS[:npv, 0:512], 4)[:, :, :n], BT[0][:npv, hs, :n],
                   ALU.add, [pSk, ("BT", 0)], [("sbS", 0)])
                pfree(pSk)
                ACT(g4(PT[0][:npv, :], 4)[:, :, :n], g4(sbS[0][:npv, :], 4)[:, :, :n], AF.Exp, [("sbS", 0)], [("PT", 0)])
                yield
            pS2, pS2k = psum()
            MM([(g4(pS2[:n, 0:512], 4)[:, :, :n], kcur[:, kvh, :n], qnT[:, hs, :n], True, True)], [kck, "qnT"], [pS2k])
            TT("dve", g4(sbS[1][:n, :], 4)[:, :, :n], g4(pS2[:n, 0:512], 4)[:, :, :n], BT[1][:n, hs, :n], ALU.add,
               [pS2k, ("BT", 1)], [("sbS", 1)])
            pfree(pS2k)
            ACT(g4(PT[1][:n, :], 4)[:, :, :n], g4(sbS[1][:n, :], 4)[:, :, :n], AF.Exp, [("sbS", 1)], [("PT", 1)])
            yield
            po, pok = psum()
            pov = po[:, 0:260].rearrange("p (g c) -> p g c", g=4)
            mms = []
            for g in range(4):
                if has_prev:
                    mms.append((pov[:n, g, :], PT[0][:npv, g * 128:g * 128 + n], vprev[:npv, kvh, 0:65], True, False))
                mms.append((pov[:n, g, :], PT[1][:n, g * 128:g * 128 + n], vcur[:n, kvh, 0:65], not has_prev, True))
            MM(mms, [("PT", 0), ("PT", 1), vck, vpk, ("vA1", l, 0), ("vA1", l, 1)], [pok])
            TT("dve", st[:n, 52:56], pov[:n, :, 64:65].rearrange("p g o -> p (g o)"),
               sinkexp[:n, l * 16 + kvh * 4:l * 16 + kvh * 4 + 4], ALU.add, [pok, "sinkexp"], ["st_a3"])
            yield
            RECIP(st[:n, 56:60], st[:n, 52:56], ["st_a3"], ["st_a4"])
            TT("dve", g4(TA2f[:n, kvh * 256:(kvh + 1) * 256], 4), pov[:n, :, 0:64],
               st[:n, 56:60].unsqueeze(2).broadcast_to([n, 4, 64]), ALU.mult, [pok, "st_a4"], [("TA2", kvh)])
            pfree(pok)
            yield
        ACT(TB2b[:n, 0:1024], TA2f[:n, 0:1024], AF.Square, TA2H, ["st_a5", "TB2b"], accum=st[:n, 60:61])
        ACT(st[:n, 62:63], st[:n, 60:61], AF.Ln, ["st_a5"], ["st_a7"], scale=1.0 / 1024, bias=EPS)
        yield
        ACT(st[:n, 63:64], st[:n, 62:63], AF.Exp, ["st_a7"], ["st_a8"], scale=-0.5)
        yield
        ACT(TB2b[:n, :], TA2f[:n, 0:1024], AF.Copy, TA2H + ["st_a8"], ["TB2b"], scale=st[:n, 63:64])
        ph, phk = psum()
        phb = ph[:].bitcast(BF16)
        TR([(phb[:, fc * 128:fc * 128 + n], TB2b[:n, fc * 128:(fc + 1) * 128], identb[:n, :n]) for fc in range(8)],
           ["TB2b", "identb"], [phk])
        TT("dve", actT[:, 8:16, tt * 128:tt * 128 + n], phb[:, 0:1024].rearrange("p (k c) -> p k c", k=8)[:, :, 0:n],
           gao_c[:, l, :].unsqueeze(2).broadcast_to([128, 8, n]), ALU.mult, [phk, "gao_c"], [("aT", tt, 1)])
        pfree(phk)
        yield

    def interleave(gens):
        gens = list(gens)
        while gens:
            for g in list(gens):
                try:
                    next(g)
                except StopIteration:
                    gens.remove(g)

    def run_pass(pinfo):
        kind, si, ti = pinfo
        if kind == "p":
            tiles = [128] * 4
            first = (ti == 0)
            last = (ti == NPASS_SEQ - 1)
        else:
            tiles = [LS] * NS
            first = True
            last = True
        ntt = len(tiles)
        full = all(n == 128 for n in tiles)

        for tt, n in enumerate(tiles):
            src = xp[si, ti * T + tt * 128: ti * T + tt * 128 + n, :] if kind == "p" else xsm[tt, 0:n, :]
            DMA("pool", x[tt][:n, :], src, [], [("x", tt)], "xl%d" % tt)

        for l in range(2):
            P.marks.append(("norm1", pinfo, l, P.nops)); P.phase = "norm1"
            prefetch_extra()
            rmsnorm_all(tiles, l, 0)
            P.marks.append(("inproj", pinfo, l, P.nops)); P.phase = "inproj"
            for j, (c0, knd, cgi) in enumerate(IN_COLS):
                wv, wk_ = next_slab(("win", l, j))
                ncw = 8 if knd == "gts" else 512
                for tt, n in enumerate(tiles):
                    pb, pbk = psum()
                    MM([(pb[:n, 0:ncw], actT[:, kc, tt * 128:tt * 128 + n], wv[:, kc, :], kc == 0, kc == KC - 1)
                        for kc in range(KC)], [wk_, ("aT", tt, 0), ("aT", tt, 1)], [pbk])
                    if knd == "big":
                        TC(alt(), bigp[:n, tt, cgi * 512:(cgi + 1) * 512], pb[:n, :], [pbk], [("big", tt * 8 + cgi)])
                    elif knd == "gts":
                        TC("dve", gts[tt][:n, :], pb[:n, 0:8], [pbk], [("gts", tt)])
                    else:
                        TC(alt(), kvt[tt][:n, :], pb[:n, :], [pbk], [("kvt", tt)])
                    pfree(pbk)
            P.marks.append(("mix", pinfo, l, P.nops)); P.phase = "mix"
            prefetch_extra()

            def stream_m():
                for tt, n in enumerate(tiles):
                    if kind == "s":
                        yield from sample_mlstm_load(l, tt)
                    elif first and tt == 0:
                        prompt_state_init(l)
                    fin = last and (tt == ntt - 1 or kind == "s")
                    if kind == "s" or tt == 0:
                        yield from mlstm_gates(l, tt, n, fin)
                    side = None
                    if kind == "p" and tt + 1 < ntt:
                        fin_n = last and (tt + 1 == ntt - 1)
                        side = mlstm_gates(l, tt + 1, tiles[tt + 1], fin_n)
                    yield from mlstm_tile(l, tt, n, fin, kind, si, side)

            def stream_a():
                for tt, n in enumerate(tiles):
                    if kind == "s":
                        yield from sample_attn_load(l, tt)
                    has_prev = not (kind == "p" and first and tt == 0)
                    par = (ti * 4 + tt) % 2 if kind == "p" else 1
                    fin = last and (tt == ntt - 1 or kind == "s")
                    yield from attn_tile(l, tt, n, has_prev, par, fin, kind, si)

            interleave([stream_m(), stream_a()])
            P.marks.append(("outproj", pinfo, l, P.nops)); P.phase = "outproj"
            for cg in range(NDC):
                wv, wk_ = next_slab(("wout", l, cg))
                for tt, n in enumerate(tiles):
                    pb, pbk = psum()
                    MM([(pb[:n, 0:DCW], actT[:, fc, tt * 128:tt * 128 + n], wv[:, fc, :], fc == 0, fc == 15)
                        for fc in range(16)], [wk_, ("aT", tt, 0), ("aT", tt, 1)], [pbk])
                    xv = x[tt][:n, cg * DCW:(cg + 1) * DCW]
                    TT("dve", xv, pb[:n, 0:DCW], xv, ALU.add, [pbk, ("x", tt)], [("x", tt)])
                    pfree(pbk)
            P.marks.append(("norm2", pinfo, l, P.nops)); P.phase = "norm2"
            prefetch_extra()
            rmsnorm_all(tiles, l, 1)
            P.marks.append(("ffn", pinfo, l, P.nops)); P.phase = "ffn"
            aT_reads = [("aT", tt, h_) for tt in range(ntt) for h_ in range(2)]
            for hf in range(NHALF):
                for j in range(UPS):
                    wv, wk_ = next_slab(("wup", l, hf, j))
                    for h4 in range(4):
                        hc = j * 4 + h4
                        pb, pbk = psum()
                        rk = ("sbS", hc % 2)
                        if full:
                            segs = [(0, ntt * 128)]
                        else:
                            segs = [(tt * 128, tiles[tt]) for tt in range(ntt)]
                        for (c0_, nn_) in segs:
                            MM([(pb[:, c0_:c0_ + nn_], wv[:, kc, h4 * 128:(h4 + 1) * 128], actT[:, kc, c0_:c0_ + nn_],
                                 kc == 0, kc == KC - 1) for kc in range(KC)], [wk_] + aT_reads, [pbk])
                        for (c0_, nn_) in segs:
                            ACT(sbS[hc % 2][:, c0_:c0_ + nn_], pb[:, c0_:c0_ + nn_], AF.Relu, [pbk], [rk])
                        pfree(pbk)
                        for (c0_, nn_) in segs:
                            rv = sbS[hc % 2][:, c0_:c0_ + nn_]
                            TT("pool", uT[:, hc, c0_:c0_ + nn_], rv, rv, ALU.mult, [rk], [("big", hc)])
                for cg in range(NDC):
                    pbs = [psum() for _ in tiles]
                    for s in range(DNS):
                        wv, wk_ = next_slab(("wdn", l, hf, cg, s))
                        for tt, n in enumerate(tiles):
                            pb, pbk = pbs[tt]
                            MM([(pb[:n, 0:DCW], uT[:, s * DNK + hc, tt * 128:tt * 128 + n], wv[:, hc, :],
                                 s == 0 and hc == 0, s == DNS - 1 and hc == DNK - 1) for hc in range(DNK)],
                               [wk_] + [("big", s * DNK + hc) for hc in range(DNK)], [pbk])
                    for tt, n in enumerate(tiles):
                        pb, pbk = pbs[tt]
                        xv = x[tt][:n, cg * DCW:(cg + 1) * DCW]
                        TT("dve", xv, pb[:n, 0:DCW], xv, ALU.add, [pbk, ("x", tt)], [("x", tt)])
                        pfree(pbk)
        for tt, n in enumerate(tiles):
            dst = y_p[si, ti * T + tt * 128: ti * T + tt * 128 + n, :] if kind == "p" else y_s[tt, 0:n, :]
            DMA("pool", dst, x[tt][:n, :], [("x", tt)], [], "so_y%d" % tt)

    P.marks.append(("const_end", P.nops))
    for pinfo in passes:
        run_pass(pinfo)
    assert P.max_ops or wstate["used"] == len(sched)
    P.wait_all("sp", lambda k: k.startswith("so_"))
    print("sbuf bytes remaining", nc.sbuf_bytes_remaining if not callable(nc.sbuf_bytes_remaining) else nc.sbuf_bytes_remaining())
    P.emit()
    es.close()
    return nc, P


_CACHE = {}


def _get_program(cfg):
    key = tuple(sorted(cfg.items()))
    if key not in _CACHE:
        _CACHE[key] = build_program(cfg)
    return _CACHE[key][0]


def run_cores(cfg, ncores, inputs):
    NSEQ = cfg["NSEQ"]; NS = cfg["NS"]
    nc = _get_program(cfg)
    oh = make_onehot()
    f = lambda a: np.ascontiguousarray(np.asarray(a, dtype=np.float32))
    shared = {k: f(inputs[k]) for k in ("rel_bias", "g_mix", "w_in", "b_i", "b_f", "g_q", "g_k", "sinks", "g_mo",
                                        "g_ao", "w_out", "g_ffn", "w_up", "w_down")}
    shared["oh"] = oh
    xp = f(inputs["x_prompt"]); xs = f(inputs["x_sample"])
    ck = f(inputs["cache_k"]); cv = f(inputs["cache_v"])
    sC = f(inputs["state_C"]); sn = f(inputs["state_n"]); sm = f(inputs["state_m"])
    in_maps = []
    for c in range(ncores):
        ps = slice(c * NSEQ, (c + 1) * NSEQ); ss = slice(c * NS, (c + 1) * NS)
        m = dict(shared)
        m["xp"] = np.ascontiguousarray(xp[ps]); m["xs"] = np.ascontiguousarray(xs[ss])
        m["ck"] = np.ascontiguousarray(ck[:, ss].reshape(2, NS, 128, 256))
        m["cv"] = np.ascontiguousarray(cv[:, ss].reshape(2, NS, 128, 256))
        m["sC"] = np.ascontiguousarray(sC[:, ss]); m["sn"] = np.ascontiguousarray(sn[:, ss])
        m["sm"] = np.ascontiguousarray(sm[:, ss])
        in_maps.append(m)
    res = run_bass_kernel_spmd(nc, in_maps, core_ids=list(range(ncores)))
    R = res.results
    cat0 = lambda k: np.concatenate([r[k] for r in R], axis=0)
    cat1 = lambda k: np.concatenate([r[k] for r in R], axis=1)
    nb = ncores * NSEQ; nsb = ncores * NS
    return (cat0("y_p"), cat0("y_s"),
            cat1("p_k").reshape(2, nb, 128, 4, 64), cat1("p_v").reshape(2, nb, 128, 4, 64),
            cat1("p_C"), cat1("p_n"), cat1("p_m"),
            cat1("s_k").reshape(2, nsb, 128, 4, 64), cat1("s_v").reshape(2, nsb, 128, 4, 64),
            cat1("s_C"), cat1("s_n"), cat1("s_m"))


def kernel(**inputs):
    outs = run_cores(full_cfg(), 8, inputs)
    return tuple(np.ascontiguousarray(o, dtype=np.float32) for o in outs)
```

```python
import math
from contextlib import ExitStack

import numpy as np
import concourse.bass as bass
import concourse.mybir as mybir
from concourse.bass_utils import run_bass_kernel_spmd

F32 = mybir.dt.float32
BF16 = mybir.dt.bfloat16
AF = mybir.ActivationFunctionType
ALU = mybir.AluOpType
AX = mybir.AxisListType

ENGS = ("pe", "act", "dve", "pool", "sp")
EPS = 1e-6
DIN = 4616
NEG = -30000.0


class Prog:
    def __init__(self, nc, es, same_engine_sync=True):
        self.nc = nc
        self.es = es
        self.q = {e: [] for e in ENGS}
        self.sems = {}
        self.count = {}
        self.waited = {e: {} for e in ENGS}
        self.lastw = {}
        self.readers = {}
        self.same_engine_sync = same_engine_sync
        self.nops = 0
        self.nwaits = 0
        import os
        self.max_ops = int(os.environ.get("KMAX", "0"))
        self.marks = []
        self.pe_log = []
        self.phase = "const"

    def sem(self, key):
        if key not in self.sems:
            self.sems[key] = self.es.enter_context(self.nc.semaphore("s_%s" % (key,)))
            self.count[key] = 0
        return self.sems[key]

    def op(self, eng, fn, reads=(), writes=(), semkey=None, inc=1):
        own = "E_" + eng
        if semkey is None:
            semkey = own
        if self.max_ops and self.nops >= self.max_ops:
            return None
        self.sem(semkey)
        deps = {}

        def add(d, kind):
            k, v = d
            if k == own:
                if eng == "pe" or not self.same_engine_sync:
                    return
            if deps.get(k, 0) < v:
                deps[k] = v

        for r in reads:
            if r in self.lastw:
                add(self.lastw[r], "raw")
        for w in writes:
            if w in self.lastw:
                add(self.lastw[w], "waw")
            for d in self.readers.get(w, ()):
                add(d, "war")
        for k, v in deps.items():
            if self.waited[eng].get(k, 0) < v:
                self.waited[eng][k] = v
                self.q[eng].append(("wait", self.sems[k], v))
                self.nwaits += 1
        self.count[semkey] += inc
        val = self.count[semkey]
        self.q[eng].append(("op", fn, self.sems[semkey], inc))
        self.nops += 1
        for w in writes:
            self.lastw[w] = (semkey, val)
            self.readers[w] = []
        for r in reads:
            self.readers.setdefault(r, []).append((semkey, val))
        return (semkey, val)

    def wait_all(self, eng, pred):
        for k, v in self.count.items():
            if not pred(k):
                continue
            if v > 0 and self.waited[eng].get(k, 0) < v:
                self.waited[eng][k] = v
                self.q[eng].append(("wait", self.sems[k], v))

    def emit(self):
        nc = self.nc
        with nc.Block() as block:
            def run(eng_name):
                def body(e):
                    for it in self.q[eng_name]:
                        if it[0] == "wait":
                            e.wait_ge(it[1], it[2])
                        else:
                            ins = it[1](e)
                            ins.then_inc(it[2], it[3])
                return body
            block.tensor(run("pe"))
            block.scalar(run("act"))
            block.vector(run("dve"))
            block.gpsimd(run("pool"))
            block.sync(run("sp"))


def t5_bucket(rel):
    half = 16
    exact = 8
    n = np.abs(rel)
    large = exact + (np.log(np.maximum(n, 1) / exact) / np.log(128 / exact) * (half - exact)).astype(np.int32)
    large = np.minimum(large, half - 1)
    return (rel > 0).astype(np.int32) * half + np.where(n < exact, n, large).astype(np.int32)


def make_onehot():
    oh = np.zeros((2, 33, 128, 128), np.float32)
    i = np.arange(128)[:, None]
    p = np.arange(128)[None, :]
    for tbl in range(2):
        rel = (p - 128 - i) if tbl == 0 else (p - i)
        bk = t5_bucket(rel)
        for b in range(32):
            oh[tbl, b] = (bk == b)
        if tbl == 0:
            oh[tbl, 32] = (p < 64) & (i >= 64)
        else:
            oh[tbl, 32] = (p >= 64) & (i < 64)
    return oh


def full_cfg():
    return dict(D=2048, DFF=8192, SEQ=2048, NSEQ=2, NS=2, LS=32, NW=2)


def build_program(cfg):
    D = cfg["D"]; DFF = cfg["DFF"]; SEQ = cfg["SEQ"]; NSEQ = cfg["NSEQ"]; NS = cfg["NS"]; LS = cfg["LS"]
    NW = cfg.get("NW", 2)
    KC = D // 128
    T = 512
    DCW = min(512, D); NDC = D // DCW
    HH = min(4096, DFF); NHALF = DFF // HH; NHC = HH // 128
    UPS = HH // 512
    DNK = min(16, NHC); DNS = NHC // DNK
    NPASS_SEQ = SEQ // T
    SCL = 128 ** -0.5

    nc = bass.Bass("TRN2", target_bir_lowering=False)

    def din(name, shape):
        return nc.dram_tensor(name, list(shape), F32, kind="ExternalInput").ap()

    def dout(name, shape):
        return nc.dram_tensor(name, list(shape), F32, kind="ExternalOutput").ap()

    xp = din("xp", [NSEQ, SEQ, D]); xsm = din("xs", [NS, LS, D])
    ck = din("ck", [2, NS, 128, 256]); cv = din("cv", [2, NS, 128, 256])
    sC = din("sC", [2, NS, 4, 256, 128]); sn = din("sn", [2, NS, 4, 128]); sm = din("sm", [2, NS, 4])
    rel_bias = din("rel_bias", [32, 16])
    g_mix = din("g_mix", [2, D]); w_in = din("w_in", [2, D, DIN])
    b_i = din("b_i", [2, 4]); b_f = din("b_f", [2, 4])
    g_q = din("g_q", [2, 64]); g_k = din("g_k", [2, 64]); sinks = din("sinks", [2, 16])
    g_mo = din("g_mo", [2, 1024]); g_ao = din("g_ao", [2, 1024])
    w_out = din("w_out", [2, 2048, D]); g_ffn = din("g_ffn", [2, D])
    w_up = din("w_up", [2, D, DFF]); w_down = din("w_down", [2, DFF, D])
    oh = din("oh", [2, 33, 128, 128])

    y_p = dout("y_p", [NSEQ, SEQ, D]); y_s = dout("y_s", [NS, LS, D])
    p_k = dout("p_k", [2, NSEQ, 128, 256]); p_v = dout("p_v", [2, NSEQ, 128, 256])
    p_C = dout("p_C", [2, NSEQ, 4, 256, 128]); p_n = dout("p_n", [2, NSEQ, 4, 128]); p_m = dout("p_m", [2, NSEQ, 4])
    s_k = dout("s_k", [2, NS, 128, 256]); s_v = dout("s_v", [2, NS, 128, 256])
    s_C = dout("s_C", [2, NS, 4, 256, 128]); s_n = dout("s_n", [2, NS, 4, 128]); s_m = dout("s_m", [2, NS, 4])

    es = ExitStack()
    P = Prog(nc, es)

    def sb(name, shape, dt):
        return es.enter_context(nc.sbuf_tensor(name, list(shape), dt))

    def ACT(out, in_, func, reads, writes, bias=None, scale=None, accum=None):
        kw_ = {}
        if bias is not None:
            kw_["bias"] = bias
        if scale is not None:
            kw_["scale"] = scale
        if accum is not None:
            kw_["accum_out"] = accum
        P.op("act", lambda e: e.activation(out=out, in_=in_, func=func, **kw_), reads=reads, writes=writes)

    def TT(eng, out, in0, in1, op, reads, writes):
        P.op(eng, lambda e: e.tensor_tensor(out=out, in0=in0, in1=in1, op=op), reads=reads, writes=writes)

    def TS(eng, out, in0, s1, s2, op0, op1, reads, writes):
        if op1 is None:
            P.op(eng, lambda e: e.tensor_scalar(out=out, in0=in0, scalar1=s1, scalar2=None, op0=op0),
                 reads=reads, writes=writes)
        else:
            P.op(eng, lambda e: e.tensor_scalar(out=out, in0=in0, scalar1=s1, scalar2=s2, op0=op0, op1=op1),
                 reads=reads, writes=writes)

    def STT(out, in0, scalar, in1, op0, op1, reads, writes):
        P.op("dve", lambda e: e.scalar_tensor_tensor(out=out, in0=in0, scalar=scalar, in1=in1, op0=op0, op1=op1),
             reads=reads, writes=writes)

    def TC(eng, out, in_, reads, writes):
        if eng == "act":
            ACT(out, in_, AF.Copy, reads, writes)
        else:
            P.op(eng, lambda e: e.tensor_copy(out=out, in_=in_), reads=reads, writes=writes)

    def RECIP(out, in_, reads, writes):
        P.op("dve", lambda e: e.reciprocal(out=out, in_=in_), reads=reads, writes=writes)

    def RECIP_LP(out, in_, reads, writes):
        def fn(e):
            with nc.allow_low_precision(reason="gate value is stored bf16 (it only multiplies a bf16 matmul operand)"):
                return e.reciprocal(out=out, in_=in_)
        P.op("dve", fn, reads=reads, writes=writes)

    def SCAN(out, d0, d1, init, op0, op1, reads, writes):
        P.op("dve", lambda e: e.tensor_tensor_scan(out=out, data0=d0, data1=d1, initial=init, op0=op0, op1=op1),
             reads=reads, writes=writes)

    def REDUCE(out, in_, reads, writes):
        P.op("dve", lambda e: e.tensor_reduce(out=out, in_=in_, axis=AX.X, op=ALU.add), reads=reads, writes=writes)

    def MEMSET(eng, ap, val, writes):
        P.op(eng, lambda e: e.memset(ap, val), writes=writes)

    def ASEL(out, in_, pattern, cmp, base, cm, reads, writes):
        P.op("pool", lambda e: e.affine_select(out=out, in_=in_, pattern=pattern, compare_op=cmp, fill=0.0,
                                               base=base, channel_multiplier=cm), reads=reads, writes=writes)

    def DMA(q, out, in_, reads, writes, semkey, slow=False):
        if slow:
            P.op(q, lambda e: e.dma_start(out=out, in_=in_, allow_slow_non_contiguous=True),
                 reads=reads, writes=writes, semkey=semkey, inc=16)
        else:
            P.op(q, lambda e: e.dma_start(out=out, in_=in_), reads=reads, writes=writes, semkey=semkey, inc=16)

    def MM(mms, reads, writes):
        mms = list(mms)
        P.pe_log.append((P.phase, len(mms)))

        def fn(e):
            last = None
            for (o, l_, r_, s0, s1) in mms:
                last = e.matmul(o, lhsT=l_, rhs=r_, start=s0, stop=s1)
            return last
        P.op("pe", fn, reads=reads, writes=writes)

    def TR(trs, reads, writes):
        trs = list(trs)
        P.pe_log.append((P.phase, len(trs)))

        def fn(e):
            last = None
            for (o, i_, idn) in trs:
                last = e.transpose(out=o, in_=i_, identity=idn)
            return last
        P.op("pe", fn, reads=reads, writes=writes)

    cp_eng = {"i": 0}

    def alt():
        cp_eng["i"] += 1
        return "act" if cp_eng["i"] % 2 else "dve"

    slabs = {}

    def reg_weight(name, l, src2d, specs):
        tot = len(specs)
        mx = max(nk * ncw for (_, _, nk, _, ncw) in specs)
        scr = nc.dram_tensor("scr_%s_%d" % (name, l), [tot, 128, mx], BF16, kind="Internal").ap()
        for i, (key, row0, nk, col0, ncw) in enumerate(specs):
            src = src2d[row0:row0 + nk * 128, col0:col0 + ncw].rearrange("(k p) c -> p k c", p=128)
            slabs[key] = dict(scr=scr[i, :, 0:nk * ncw], nk=nk, ncw=ncw, src=src, cast=False)

    IN_COLS = [(0, "big", 0), (512, "big", 1), (1024, "big", 2), (1536, "big", 3), (2048, "big", 4),
               (2560, "big", 5), (3080, "big", 6), (3592, "big", 7), (4104, "kvt", None), (3072, "gts", None)]
    for l in range(2):
        reg_weight("win", l, w_in[l], [(("win", l, j), 0, KC, c0, (8 if kind == "gts" else 512))
                                       for j, (c0, kind, _) in enumerate(IN_COLS)])
        reg_weight("wout", l, w_out[l], [(("wout", l, cg), 0, 16, cg * DCW, DCW) for cg in range(NDC)])
        reg_weight("wup", l, w_up[l], [(("wup", l, hf, j), 0, KC, hf * HH + j * 512, 512)
                                       for hf in range(NHALF) for j in range(UPS)])
        reg_weight("wdn", l, w_down[l], [(("wdn", l, hf, cg, s), hf * HH + s * DNK * 128, DNK, cg * DCW, DCW)
                                         for hf in range(NHALF) for cg in range(NDC) for s in range(DNS)])

    passes = []
    for si in range(NSEQ):
        for ti in range(NPASS_SEQ):
            passes.append(("p", si, ti))
    if NS > 0:
        passes.append(("s", 0, 0))
    sched = []
    for ps_ in passes:
        for l in range(2):
            for j in range(len(IN_COLS)):
                sched.append(("win", l, j))
            for cg in range(NDC):
                sched.append(("wout", l, cg))
            for hf in range(NHALF):
                for j in range(UPS):
                    sched.append(("wup", l, hf, j))
                for cg in range(NDC):
                    for s in range(DNS):
                        sched.append(("wdn", l, hf, cg, s))
    wbuf = [sb("wbuf%d" % s, [128, 16 * 512], BF16) for s in range(NW)]
    wslots = [w_[:] for w_ in wbuf]
    per_pass = len(sched) // len(passes)
    extra_slot = (NS > 0) and (4 * D >= 16 * 512) and NW == 2
    j0s = len(sched) - per_pass if NS > 0 else len(sched)

    def slot_of(j):
        if extra_slot and j >= j0s:
            return [j0s % 2, (j0s + 1) % 2, 2][(j - j0s) % 3]
        return j % NW

    def nslots(j):
        return 3 if (extra_slot and j >= j0s) else NW
    wstate = {"loaded": 0, "used": 0, "cast": 0, "ncast": 0}
    CL = 28
    NCS = 32

    def cast_upto(j):
        while wstate["cast"] <= min(j, len(sched) - 1):
            key = sched[wstate["cast"]]
            wstate["cast"] += 1
            sl = slabs[key]
            if sl["cast"]:
                continue
            sl["cast"] = True
            ck_ = "cast%d" % (wstate["ncast"] % NCS)
            wstate["ncast"] += 1
            DMA("pool", sl["scr"].rearrange("p (k c) -> p k c", k=sl["nk"]), sl["src"], [], [("scr", key)], ck_)

    def load_next():
        j = wstate["loaded"]
        key = sched[j]
        sl = slabs[key]
        s = slot_of(j)
        wr = [("w", s)] + ([("x", 2), ("x", 3)] if s == 2 else [])
        DMA("sp", wslots[s][:, 0:sl["nk"] * sl["ncw"]], sl["scr"], [("scr", key)], wr, "wl%d" % s)
        wstate["loaded"] += 1

    def prefetch_extra():
        j = wstate["used"]
        if j >= len(sched):
            return
        cast_upto(j + nslots(j) + CL)
        while wstate["loaded"] < min(len(sched), j + nslots(j)):
            load_next()

    def next_slab(key):
        j = wstate["used"]
        assert sched[j] == key, (sched[j], key)
        cast_upto(j + NW + CL)
        while wstate["loaded"] < min(len(sched), j + nslots(j)):
            load_next()
        wstate["used"] += 1
        sl = slabs[key]
        s = slot_of(j)
        return wslots[s][:, 0:sl["nk"] * sl["ncw"]].rearrange("p (k c) -> p k c", k=sl["nk"]), ("w", s)

    banks = [es.enter_context(nc.psum_tensor("ps%d" % b, [128, 512], F32)) for b in range(8)]
    from collections import deque
    pfree_list = deque(range(7))

    def psum():
        assert pfree_list, "out of PSUM banks"
        b = pfree_list.popleft()
        return banks[b], ("ps", b)

    def pfree(key):
        assert key[1] not in pfree_list
        pfree_list.append(key[1])
    pm = banks[7]
    pmk = ("ps", 7)

    x0_ = sb("x0", [128, D], F32)
    x1_ = sb("x1", [128, D], F32)
    x23 = sb("x23", [128, 2 * D], F32)
    x = [x0_[:], x1_[:], x23[:, 0:D], x23[:, D:2 * D]]
    if extra_slot:
        wslots.append(x23[:].bitcast(BF16)[:, 0:16 * 512])
    actT = sb("actT", [128, 16, 512], BF16)
    big = sb("big", [128, 16384], BF16)
    bigp = big[:].rearrange("p (t c) -> p t c", t=4)
    uT = big[:].rearrange("p (h n) -> p h n", n=512)
    kvt = [sb("kvt%d" % t, [128, 512], F32) for t in range(4)]
    gts = [sb("gts%d" % t, [128, 8], F32) for t in range(4)]
    BT = [sb("BT%d" % t, [128, 16, 128], F32) for t in range(2)]
    xs_t = [sb("xs%d" % i, [128, D], BF16) for i in range(2)]
    identb = sb("identb", [128, 128], BF16)
    identf = sb("identf", [128, 128], F32)
    ones4 = sb("ones4", [4, 128], F32)
    zer4 = sb("zer4", [4, 128], F32)
    selh = sb("selh", [4, 4, 128], F32)
    gcol = sb("gcol", [128, 2, 2, 16], F32)
    gmo_c = sb("gmo_c", [128, 2, 8], F32)
    gao_c = sb("gao_c", [128, 2, 8], F32)
    gq_c = sb("gq_c", [64, 2], F32)
    gk_c = sb("gk_c", [64, 2], F32)
    gq8_c = sb("gq8_c", [64, 2], F32)
    gk_bc = sb("gk_bc", [128, 2, 64], F32)
    bi_c = sb("bi_c", [4, 2], F32)
    nbf_c = sb("nbf_c", [4, 2], F32)
    sinkexp = sb("sinkexp", [128, 32], F32)
    rbx = sb("rbx", [33, 16], F32)
    st = sb("st", [128, 64], F32)
    Cst = [sb("C%d" % l, [128, 4, 257], F32) for l in range(2)]
    Cbf = [sb("Cbf%d" % l, [128, 4, 258], BF16) for l in range(2)]
    kTpp = [[sb("kTpp%d_%d" % (l, i), [64, 4, 128], BF16) for i in range(2)] for l in range(2)]
    vApp = [[sb("vApp%d_%d" % (l, i), [128, 4, 66], BF16) for i in range(2)] for l in range(2)]
    carL = [sb("carL%d" % l, [4, 1], F32) for l in range(2)]
    carM = [sb("carM%d" % l, [4, 1], F32) for l in range(2)]
    g_ig = sb("g_ig", [4, 128], F32); g_l = sb("g_l", [4, 128], F32); g_Lc = sb("g_Lc", [4, 128], F32)
    g_A = sb("g_A", [4, 128], F32); g_mu = sb("g_mu", [4, 128], F32); g_g = sb("g_g", [4, 128], F32)
    g_md = sb("g_md", [4, 128], F32); g_wk = sb("g_wk", [4, 128], F32)
    g_s = sb("g_s", [4, 8], F32)
    dd4 = sb("dd4", [4, 4], F32)
    gpt = sb("gpt", [128, 16], F32)
    decbc = sb("decbc", [128, 4], F32)
    qT = sb("qT", [128, 4, 128], BF16); kT = sb("kT", [128, 4, 128], BF16)
    vext = sb("vext", [128, 4, 258], BF16)
    ET = [sb("ET%d" % i, [128, 128], F32) for i in range(2)]
    maskB = sb("maskB", [128, 128], F32)
    wT = [sb("wT_%d" % i, [128, 128], BF16) for i in range(2)]
    kw = [sb("kw%d" % i, [128, 128], BF16) for i in range(2)]
    inter_s0 = sb("inter_s0", [128, 257], F32)
    TA = sb("TA", [128, 4, 257], F32)
    TAf = TA[:].rearrange("p h c -> p (h c)")
    TB1 = sb("TB1", [128, 1024], BF16)
    TB2 = sb("TB2", [128, 1024], BF16)
    TA2 = sb("TA2", [128, 1024], F32)
    TA2f = TA2[:]
    TB1b = sb("TB1b", [128, 1024], BF16)
    inter_s = [inter_s0[:], TB2[:].bitcast(F32)[:, 0:257]]
    inter_k = [("inter_s", 0), "TB2"]
    TB2b = sb("TB2b", [128, 1024], BF16)
    ks = sb("ks", [128, 256], BF16)
    knf = sb("knf", [128, 256], F32)
    qnT = sb("qnT", [64, 16, 128], BF16)
    sbS = [sb("sbS%d" % i, [128, 512], F32) for i in range(2)]
    PT = [sb("PT%d" % i, [128, 512], BF16) for i in range(2)]

    CH = lambda l: [("C", l, h) for h in range(4)]
    CBH = lambda l: [("Cbf", l, h) for h in range(4)]

    MEMSET("pool", identf[:], 1.0, ["identf"])
    ASEL(identf[:], identf[:], [[-1, 128]], ALU.is_equal, 0, 1, ["identf"], ["identf"])
    TC("pool", identb[:], identf[:], ["identf"], ["identb"])
    MEMSET("pool", ones4[:], 1.0, ["ones4"])
    MEMSET("pool", maskB[:], 30000.0, ["maskB"])
    ASEL(maskB[:], maskB[:], [[-1, 128]], ALU.is_gt, 0, 1, ["maskB"], ["maskB"])
    MEMSET("pool", zer4[:], 0.0, ["zer4"])
    MEMSET("pool", selh[:], 1.0, ["selh"])
    for h in range(4):
        ASEL(selh[:, h, :], selh[:, h, :], [[0, 128]], ALU.is_equal, -h, 1, ["selh"], ["selh"])
    for l in range(2):
        for pi_ in range(2):
            MEMSET("pool", vApp[l][pi_][:, :, 64:65], 1.0, [("vA1", l, pi_)])
    MEMSET("pool", vext[:, :, 256:257], 1.0, ["vext1"])
    MEMSET("pool", rbx[:], NEG, ["rbx"])
    cq = "cst"
    cres = ["gcol", "gmo_c", "gao_c", "gq_c", "gk_c", "gk_bc", "bi_c", "nbf_c", "sinkexp", "rbx"]
    for l in range(2):
        DMA("sp", gcol[:, l, 0, 0:KC], g_mix[l].rearrange("(k p) -> p k", p=128), [], [("cst_tmp", "gcol_mix", l)], cq, slow=True)
        DMA("sp", gcol[:, l, 1, 0:KC], g_ffn[l].rearrange("(k p) -> p k", p=128), [], [("cst_tmp", "gcol", l)], cq, slow=True)
        DMA("sp", gmo_c[:, l, :], g_mo[l].rearrange("(k p) -> p k", p=128), [], [("cst_tmp", "gmo_c", l)], cq, slow=True)
        DMA("sp", gao_c[:, l, :], g_ao[l].rearrange("(k p) -> p k", p=128), [], [("cst_tmp", "gao_c", l)], cq, slow=True)
        DMA("sp", gq_c[:, l:l + 1], g_q[l].rearrange("(p o) -> p o", o=1), [], [("cst_tmp", "gq_c", l)], cq, slow=True)
        DMA("sp", gk_c[:, l:l + 1], g_k[l].rearrange("(p o) -> p o", o=1), [], [("cst_tmp", "gk_c", l)], cq, slow=True)
        DMA("sp", gk_bc[:, l, :], g_k[l:l + 1, :].broadcast_to([128, 64]), [], [("cst_tmp", "gk_bc", l)], cq, slow=True)
        DMA("sp", bi_c[:, l:l + 1], b_i[l].rearrange("(p o) -> p o", o=1), [], [("cst_tmp", "bi_c", l)], cq, slow=True)
        DMA("sp", nbf_c[:, l:l + 1], b_f[l].rearrange("(p o) -> p o", o=1), [], [("cst_tmp", "nbf_c", l)], cq, slow=True)
        DMA("sp", sinkexp[:, l * 16:(l + 1) * 16], sinks[l:l + 1, :].broadcast_to([128, 16]), [], [("cst_tmp", "sinkexp", l)], cq,
            slow=True)
    DMA("sp", rbx[0:32, :], rel_bias, ["rbx"], [("cst_tmp", "rbx", 0)], cq)
    for r in cres:
        P.lastw[r] = (cq, P.count[cq])
    ACT(sinkexp[:], sinkexp[:], AF.Exp, ["sinkexp"], ["sinkexp"])
    TS("dve", gq8_c[:], gq_c[:], 0.125, None, ALU.mult, None, ["gq_c"], ["gq8_c"])
    TS("dve", nbf_c[:], nbf_c[:], -1.0, None, ALU.mult, None, ["nbf_c"], ["nbf_c"])
    for tbl in range(2):
        for i0 in range(0, 128, 32):
            pb, pbk = psum()
            for sub in range(8):
                bi_ = sub % 2
                buf = sbS[bi_][0:33, :].rearrange("p (a k) -> p a k", a=4)
                bk = ("sbS", bi_)
                DMA("sp", buf, oh[tbl, :, i0 + sub * 4:i0 + sub * 4 + 4, :], [], [bk], "ohl%d" % bi_)
                MM([(pb[:, (sub * 4 + ii) * 16:(sub * 4 + ii + 1) * 16], buf[:, ii, :], rbx[:, :], True, True)
                    for ii in range(4)], [bk, "rbx"], [pbk])
            TC("dve", BT[tbl][:, :, i0:i0 + 32].rearrange("p h i -> p i h"),
               pb[:, :].rearrange("p (i h) -> p i h", h=16), [pbk], [("BT", tbl)])
            pfree(pbk)

    def rmsnorm_all(tiles, l, which):
        ntt = len(tiles)
        n = tiles[0]
        for tt in range(ntt):
            xs = xs_t[tt % 2]
            ACT(xs[:n, :], x[tt][:n, :], AF.Square, [("x", tt)], ["st_n0", ("xs", tt % 2)], accum=st[:n, tt:tt + 1])
        ACT(st[:n, 0:ntt], st[:n, 0:ntt], AF.Ln, ["st_n0"], ["st_n0"], scale=1.0 / D, bias=EPS)
        ACT(st[:n, 4:4 + ntt], st[:n, 0:ntt], AF.Exp, ["st_n0"], ["st_n3"], scale=-0.5)
        for tt in range(ntt):
            xs = xs_t[tt % 2]
            xk = ("xs", tt % 2)
            ACT(xs[:n, :], x[tt][:n, :], AF.Copy, [("x", tt), "st_n3"], [xk], scale=st[:n, 4 + tt:5 + tt])
            for f0 in range(0, KC, 4):
                nf = min(4, KC - f0)
                pb, pbk = psum()
                pbb = pb[:].bitcast(BF16)
                TR([(pbb[:, j * 128:j * 128 + n], xs[:n, (f0 + j) * 128:(f0 + j + 1) * 128], identb[:n, :n])
                    for j in range(nf)], [xk, "identb"], [pbk])
                TT("dve", actT[:, f0:f0 + nf, tt * 128:tt * 128 + n],
                   pbb[:, 0:nf * 128].rearrange("p (k c) -> p k c", k=nf)[:, :, 0:n],
                   gcol[:, l, which, f0:f0 + nf].unsqueeze(2).broadcast_to([128, nf, n]), ALU.mult,
                   [pbk, "gcol"], [("aT", tt, 0), ("aT", tt, 1)])
                pfree(pbk)

    def prompt_state_init(l):
        MEMSET("pool", Cst[l][:], 0.0, CH(l))
        MEMSET("pool", Cbf[l][:], 0.0, CBH(l))
        MEMSET("pool", carL[l][:], 0.0, [("carL", l)])
        MEMSET("pool", carM[l][:], 0.0, [("carM", l)])

    TAH = [("TA", h) for h in range(4)]
    TA2H = [("TA2", h) for h in range(4)]

    def sample_mlstm_load(l, j):
        stg = TAf[:, 0:1024].rearrange("p (a k) -> p a k", a=8)
        DMA("pool", stg, sC[l, j].rearrange("h (c p) k -> p (h c) k", p=128), [], TAH, "sldC")
        yield
        for h0 in range(0, 4, 2):
            pb, pbk = psum()
            TR([(pb[:, (hh * 2 + c2) * 128:(hh * 2 + c2 + 1) * 128], stg[:, (h0 + hh) * 2 + c2, :], identf[:])
                for hh in range(2) for c2 in range(2)], TAH + ["identf"], [pbk])
            TC("dve", Cst[l][:, h0:h0 + 2, 0:256], pb[:, :].rearrange("p (h v) -> p h v", h=2), [pbk], CH(l))
            pfree(pbk)
            yield
        DMA("pool", Cst[l][:, :, 256:257].rearrange("p h o -> p (h o)"), sn[l, j].rearrange("h k -> k h"),
            CH(l), CH(l), "sldn", slow=True)
        ACT(Cbf[l][:, :, 0:257], Cst[l][:], AF.Copy, CH(l), CBH(l))
        MEMSET("pool", carL[l][:], 0.0, [("carL", l)])
        DMA("pool", carM[l][:], sm[l, j].rearrange("(p o) -> p o", o=1), [], [("carM", l)], "sldm", slow=True)
        yield

    def sample_attn_load(l, j):
        stk = TA2f[:, 0:256]
        DMA("pool", stk, ck[l, j], TA2H, TA2H, "sldk")
        TC("dve", ks[:, :], stk, TA2H, ["ks"])
        yield
        pb, pbk = psum()
        pbb = pb[:].bitcast(BF16)
        TR([(pbb[0:64, kh * 128:(kh + 1) * 128], ks[:, kh * 64:(kh + 1) * 64], identb[:]) for kh in range(4)],
           ["ks", "identb"], [pbk])
        ACT(kTpp[l][0][:], pbb[0:64, 0:512].rearrange("p (h n) -> p h n", h=4), AF.Copy, [pbk], [("kTpp", l, 0)])
        pfree(pbk)
        yield
        stv = TA2f[:, 256:512]
        DMA("pool", stv, cv[l, j], TA2H, TA2H, "sldv")
        TC("dve", vApp[l][0][:, :, 0:64], stv.rearrange("p (h d) -> p h d", h=4), TA2H + [("vA1", l, 0)], [("vA", l, 0)])
        DMA("pool", s_k[l, j, 0:128 - LS, :], ck[l, j, LS:128, :], [], [], "so_cp")
        DMA("pool", s_v[l, j, 0:128 - LS, :], cv[l, j, LS:128, :], [], [], "so_cp")
        yield

    def mlstm_gates(l, tt, n, fin):
        cL = ("carL", l); cM = ("carM", l)
        pg, pgk = psum()
        TR([(pg[0:4, 0:n], gts[tt][:n, 0:4], identf[:n, :n]),
            (pg[0:4, 128:128 + n], gts[tt][:n, 4:8], identf[:n, :n])], [("gts", tt), "identf"], [pgk])
        ACT(g_ig[:, :n], pg[0:4, 0:n], AF.Identity, [pgk, "bi_c"], ["g_ig"], bias=bi_c[:, l:l + 1])
        ACT(g_l[:, :n], pg[0:4, 128:128 + n], AF.Exp, [pgk, "nbf_c"], ["g_l"], scale=-1.0, bias=nbf_c[:, l:l + 1])
        pfree(pgk)
        yield
        ACT(g_l[:, :n], g_l[:, :n], AF.Ln, ["g_l"], ["g_l"], bias=1.0)
        SCAN(g_Lc[:, :n], g_l[:, :n], zer4[:, :n], carL[l][:, 0:1], ALU.add, ALU.add, ["g_l", "zer4", cL], ["g_Lc"])
        yield
        TT("dve", g_A[:, :n], g_ig[:, :n], g_Lc[:, :n], ALU.add, ["g_ig", "g_Lc"], ["g_A"])
        SCAN(g_mu[:, :n], g_A[:, :n], zer4[:, :n], carM[l][:, 0:1], ALU.max, ALU.add, ["g_A", "zer4", cM], ["g_mu"])
        yield
        ACT(g_g[:, :n], g_mu[:, :n], AF.Exp, ["g_mu", cM], ["g_g"], scale=-1.0, bias=carM[l][:, 0:1])
        TT("dve", g_md[:, :n], g_Lc[:, :n], g_mu[:, :n], ALU.subtract, ["g_Lc", "g_mu"], ["g_md"])
        yield
        ACT(g_md[:, :n], g_md[:, :n], AF.Exp, ["g_md"], ["g_md"])
        TS("dve", g_s[:, 0:1], g_mu[:, n - 1:n], -1.0, None, ALU.mult, None, ["g_mu"], ["g_s0"])
        yield
        ACT(g_wk[:, :n], g_A[:, :n], AF.Exp, ["g_A", "g_s0"], ["g_wk"], bias=g_s[:, 0:1])
        ACT(g_s[:, 1:2], carM[l][:, 0:1], AF.Exp, [cM, "g_s0"], ["g_s1"], bias=g_s[:, 0:1])
        if fin:
            TT("dve", g_s[:, 2:3], g_mu[:, n - 1:n], g_Lc[:, n - 1:n], ALU.subtract, ["g_mu", "g_Lc"], ["g_s2"])
        TC("dve", carL[l][:, 0:1], g_Lc[:, n - 1:n], ["g_Lc"], [cL])
        TC("dve", carM[l][:, 0:1], g_mu[:, n - 1:n], ["g_mu"], [cM])
        yield

    def mlstm_tile(l, tt, n, fin, kind, si, side=None):
        mqv = bigp[:, tt, 0:512]; mkv = bigp[:, tt, 512:1024]
        mvv = bigp[:, tt, 1024:2048]; mov = bigp[:, tt, 2048:3072]
        rbig = [("big", tt * 8 + i) for i in range(6)]
        sd = [side]

        def tick():
            if sd[0] is not None:
                try:
                    next(sd[0])
                except StopIteration:
                    sd[0] = None
        pt, ptk = psum()
        TR([(pt[:n, 0:4], g_A[:, :n], identf[0:4, 0:4]), (pt[:n, 4:8], g_g[:, :n], identf[0:4, 0:4]),
            (pt[:n, 8:12], g_md[:, :n], identf[0:4, 0:4]), (pt[:n, 12:16], g_wk[:, :n], identf[0:4, 0:4])],
           ["g_A", "g_g", "g_md", "g_wk", "identf"], [ptk])
        TC("dve", gpt[:n, :], pt[:n, 0:16], [ptk], ["gpt"])
        pfree(ptk)
        TS("dve", dd4[:], identf[0:4, 0:4], g_s[:, 1:2], None, ALU.mult, None, ["identf", "g_s1"], ["dd4"])
        yield
        pd, pdk = psum()
        MM([(pd[:, 0:4], ones4[:, :], dd4[:, :], True, True)], ["ones4", "dd4"], [pdk])
        TC("dve", decbc[:], pd[:, 0:4], [pdk], ["decbc"])
        pfree(pdk)
        mm_ = []
        for h in range(4):
            mm_.append((pm[:, h * 128:h * 128 + n], selh[:, h, :], g_mu[:, :n], True, False))
            mm_.append((pm[:n, h * 128:h * 128 + n], identf[:n, :n], maskB[:n, :n], False, True))
        MM(mm_, ["selh", "g_mu", "identf", "maskB"], [pmk])
        yield
        pq, pqk = psum()
        pqb = pq[:].bitcast(BF16)
        TR([(pqb[:, h * 128:h * 128 + n], mqv[:n, h * 128:(h + 1) * 128], identb[:n, :n]) for h in range(4)] +
           [(pqb[:, 512 + h * 128:512 + h * 128 + n], mkv[:n, h * 128:(h + 1) * 128], identb[:n, :n])
            for h in range(4)], rbig[0:2] + ["identb"], [pqk])
        ACT(qT[:, :, :n], pqb[:, 0:512].rearrange("p (h c) -> p h c", h=4)[:, :, :n], AF.Copy, [pqk], ["qT"])
        ACT(kT[:, :, :n], pqb[:, 512:1024].rearrange("p (h c) -> p h c", h=4)[:, :, :n], AF.Copy, [pqk], ["kT"])
        pfree(pqk)
        tick()
        yield
        TC("act", vext[:n, :, 0:256], mvv[:n, :].rearrange("p (h v) -> p h v", h=4), rbig[2:4] + ["vext1"], ["vext"])
        ACT(TB1[:n, :], mov[:n, :], AF.Exp, rbig[4:6], ["TB1"], scale=-1.0)
        TS("dve", TB1[:n, :], TB1[:n, :], 1.0, None, ALU.add, None, ["TB1"], ["TB1"])
        RECIP_LP(TB1[:n, :], TB1[:n, :], ["TB1"], ["TB1"])
        tick()
        yield
        def head_steps(h):
            i2 = h % 2
            pS, pSk = psum()
            MM([(pS[:n, 0:n], kT[:, h, :n], qT[:, h, :n], True, True)], ["kT", "qT"], [pSk])
            ACT(ET[i2][:n, :n], pm[:n, h * 128:h * 128 + n], AF.Exp, [pmk, "gpt"], [("ET", i2)], scale=-1.0,
                bias=gpt[:n, h:h + 1])
            yield
            STT(wT[i2][:n, :n], pS[:n, 0:n], SCL, ET[i2][:n, :n], ALU.mult, ALU.mult, [pSk, ("ET", i2)], [("wT", i2)])
            pfree(pSk)
            TS("dve", kw[i2][:n, :], mkv[:n, h * 128:(h + 1) * 128], gpt[:n, 12 + h:13 + h], SCL, ALU.mult, ALU.mult,
               [rbig[1], "gpt"], [("kw", i2)])
            yield
            pJ, pJk = psum()
            MM([(pJ[:n, 0:257], qT[:, h, :n], Cbf[l][:, h, 0:257], True, True)], ["qT", ("Cbf", l, h)], [pJk])
            ACT(inter_s[i2][:n, :], pJ[:n, 0:257], AF.Copy, [pJk, "gpt"], [inter_k[i2]], scale=gpt[:n, 4 + h:5 + h])
            pfree(pJk)
            yield
            pI, pIk = psum()
            MM([(pI[:n, 0:257], wT[i2][:n, :n], vext[:n, h, 0:257], True, True)], [("wT", i2), "vext"], [pIk])
            TT("dve", TA[:n, h, :], pI[:n, 0:257], inter_s[i2][:n, :], ALU.add, [pIk, inter_k[i2]], [("TA", h)])
            pfree(pIk)
            yield
            pC, pCk = psum()
            MM([(pC[:, 0:257], kw[i2][:n, :], vext[:n, h, 0:257], True, True)], [("kw", i2), "vext"], [pCk])
            STT(Cst[l][:, h, :], Cst[l][:, h, :], decbc[:, h:h + 1], pC[:, 0:257], ALU.mult, ALU.add,
                [pCk, "decbc", ("C", l, h)], [("C", l, h)])
            pfree(pCk)
            TC("pool", Cbf[l][:, h, 0:257], Cst[l][:, h, :], [("C", l, h)], [("Cbf", l, h)])
            yield

        for hp in ((0, 1), (2, 3)):
            ga, gb = head_steps(hp[0]), head_steps(hp[1])
            alive = [ga, gb]
            while alive:
                for g_ in list(alive):
                    try:
                        next(g_)
                    except StopIteration:
                        alive.remove(g_)
                tick()
                yield
        hn_den = TA[:n, :, 256:257].rearrange("p h o -> p (h o)")
        ACT(st[:n, 8:12], hn_den, AF.Abs, TAH, ["st_m0"])
        TT("dve", st[:n, 8:12], st[:n, 8:12], gpt[:n, 8:12], ALU.max, ["st_m0", "gpt"], ["st_m0"])
        tick()
        yield
        RECIP(st[:n, 12:16], st[:n, 8:12], ["st_m0"], ["st_m1"])
        for h in range(4):
            ACT(TB2[:n, h * 256:(h + 1) * 256], TA[:n, h, 0:256], AF.Square, [("TA", h)], [("st_m2", h), "TB2"],
                accum=st[:n, 16 + h:17 + h])
            if h % 2:
                tick()
                yield
        TT("dve", st[:n, 20:24], st[:n, 12:16], st[:n, 12:16], ALU.mult, ["st_m1"], ["st_m3"])
        TT("dve", st[:n, 20:24], st[:n, 20:24], st[:n, 16:20], ALU.mult, ["st_m3"] + [("st_m2", h) for h in range(4)],
           ["st_m3"])
        tick()
        yield
        ACT(st[:n, 24:28], st[:n, 20:24], AF.Ln, ["st_m3"], ["st_m4"], scale=1.0 / 256, bias=EPS)
        ACT(st[:n, 28:32], st[:n, 24:28], AF.Exp, ["st_m4"], ["st_m5"], scale=-0.5)
        tick()
        yield
        TT("dve", st[:n, 28:32], st[:n, 28:32], st[:n, 12:16], ALU.mult, ["st_m5", "st_m1"], ["st_m5"])
        tick()
        yield
        for h in range(4):
            STT(TB2[:n, h * 256:(h + 1) * 256], TA[:n, h, 0:256], st[:n, 28 + h:29 + h], TB1[:n, h * 256:(h + 1) * 256],
                ALU.mult, ALU.mult, [("TA", h), "st_m5", "TB1"], ["TB2"])
            if h % 2:
                tick()
                yield
        ph, phk = psum()
        phb = ph[:].bitcast(BF16)
        TR([(phb[:, fc * 128:fc * 128 + n], TB2[:n, fc * 128:(fc + 1) * 128], identb[:n, :n]) for fc in range(8)],
           ["TB2", "identb"], [phk])
        TT("dve", actT[:, 0:8, tt * 128:tt * 128 + n], phb[:, 0:1024].rearrange("p (k c) -> p k c", k=8)[:, :, 0:n],
           gmo_c[:, l, :].unsqueeze(2).broadcast_to([128, 8, n]), ALU.mult, [phk, "gmo_c"], [("aT", tt, 0)])
        pfree(phk)
        tick()
        yield
        if fin:
            b = si if kind == "p" else tt
            oC, on_, om = (p_C, p_n, p_m) if kind == "p" else (s_C, s_n, s_m)
            sk_ = "so_m_%s%d_%d" % (kind, l, b)
            stg = TAf[:, 0:1024].rearrange("p (a k) -> p a k", a=8)
            for h0 in range(0, 4, 2):
                pb, pbk = psum()
                TR([(pb[:, (hh * 2 + c2) * 128:(hh * 2 + c2 + 1) * 128], Cst[l][:, h0 + hh, c2 * 128:(c2 + 1) * 128],
                     identf[:]) for hh in range(2) for c2 in range(2)], CH(l) + ["identf"], [pbk])
                TC("dve", stg[:, h0 * 2:h0 * 2 + 4, :], pb[:, :].rearrange("p (a k) -> p a k", a=4), [pbk], TAH)
                pfree(pbk)
                tick()
                yield
            DMA("pool", oC[l, b].rearrange("h (c p) k -> p (h c) k", p=128), stg, TAH, [], sk_)
            DMA("pool", on_[l, b].rearrange("h k -> k h"), Cst[l][:, :, 256:257].rearrange("p h o -> p (h o)"),
                CH(l), [], sk_, slow=True)
            DMA("pool", om[l, b].rearrange("(p o) -> p o", o=1), g_s[:, 2:3], ["g_s2"], [], sk_, slow=True)
            tot = (sk_, P.count.get(sk_, 0))
            for r in TAH + CH(l) + ["g_s2"]:
                P.readers.setdefault(r, []).append(tot)
            tick()
            yield
        while sd[0] is not None:
            tick()
            yield

    def attn_tile(l, tt, n, has_prev, par, fin, kind, si):
        aqv = bigp[:, tt, 3072:4096]
        raq = [("big", tt * 8 + 6), ("big", tt * 8 + 7)]
        kcur = kTpp[l][par]; kprev = kTpp[l][1 - par]
        vcur = vApp[l][par]; vprev = vApp[l][1 - par]
        kck = ("kTpp", l, par); kpk = ("kTpp", l, 1 - par)
        vck = ("vA", l, par); vpk = ("vA", l, 1 - par)
        npv = 128
        g4 = lambda ap2, m: ap2.rearrange("p (g c) -> p g c", g=m)
        ACT(TA2f[:n, 0:1024], aqv[:n, :], AF.Square, raq, TA2H)
        REDUCE(st[:n, 32:48], g4(TA2f[:n, 0:1024], 16), TA2H, ["st_a0"])
        yield
        ACT(TA2f[:n, 0:256], kvt[tt][:n, 0:256], AF.Square, [("kvt", tt)], TA2H)
        REDUCE(st[:n, 48:52], g4(TA2f[:n, 0:256], 4), TA2H, ["st_a1"])
        yield
        ACT(st[:n, 32:52], st[:n, 32:52], AF.Ln, ["st_a0", "st_a1"], ["st_a2"], scale=1.0 / 64, bias=EPS)
        ACT(st[:n, 32:52], st[:n, 32:52], AF.Exp, ["st_a2"], ["st_a2"], scale=-0.5)
        yield
        TT("dve", g4(TB1b[:n, :], 16), g4(aqv[:n, :], 16), st[:n, 32:48].unsqueeze(2).broadcast_to([n, 16, 64]),
           ALU.mult, raq + ["st_a2"], ["TB1b"])
        yield
        TT("dve", g4(knf[:n, :], 4), g4(kvt[tt][:n, 0:256], 4), st[:n, 48:52].unsqueeze(2).broadcast_to([n, 4, 64]),
           ALU.mult, [("kvt", tt), "st_a2"], ["knf"])
        TC("dve", ks[:n, :], knf[:n, :], ["knf"], ["ks"])
        yield
        for h0 in (0, 8):
            pb, pbk = psum()
            pbb = pb[:].bitcast(BF16)
            TR([(pbb[0:64, j * 128:j * 128 + n], TB1b[:n, (h0 + j) * 64:(h0 + j + 1) * 64], identb[:n, :n])
                for j in range(8)], ["TB1b", "identb"], [pbk])
            ACT(qnT[:, h0:h0 + 8, :n], g4(pbb[0:64, 0:1024], 8)[:, :, :n], AF.Copy, [pbk, "gq8_c"], ["qnT"],
                scale=gq8_c[:, l:l + 1])
            pfree(pbk)
            yield
        pb, pbk = psum()
        pbb = pb[:].bitcast(BF16)
        TR([(pbb[0:64, j * 128:j * 128 + n], ks[:n, j * 64:(j + 1) * 64], identb[:n, :n]) for j in range(4)],
           ["ks", "identb"], [pbk])
        ACT(kcur[:, :, :n], g4(pbb[0:64, 0:512], 4)[:, :, :n], AF.Copy, [pbk, "gk_c"], [kck], scale=gk_c[:, l:l + 1])
        pfree(pbk)
        TC("pool", vcur[:n, :, 0:64], g4(kvt[tt][:n, 256:512], 4), [("kvt", tt), ("vA1", l, par)], [vck])
        yield
        if fin:
            b = si if kind == "p" else tt
            ok_, ov_ = (p_k, p_v) if kind == "p" else (s_k, s_v)
            r0 = 0 if kind == "p" else 128 - n
            sk_ = "so_a_%s%d_%d" % (kind, l, b)
            TT("dve", g4(knf[:n, :], 4), g4(knf[:n, :], 4), gk_bc[:n, l, :].unsqueeze(1).broadcast_to([n, 4, 64]),
               ALU.mult, ["knf", "gk_bc"], ["knf"])
            DMA("pool", ok_[l, b, r0:r0 + n, :], knf[:n, :], ["knf"], [], sk_)
            DMA("pool", ov_[l, b, r0:r0 + n, :], kvt[tt][:n, 256:512], [("kvt", tt)], [], sk_)
            tot = (sk_, P.count.get(sk_, 0))
            for r in ["knf", ("kvt", tt)]:
                P.readers.setdefault(r, []).append(tot)
            yield
        for kvh in range(4):
            hs = slice(kvh * 4, kvh * 4 + 4)
            if has_prev:
                pS, pSk = psum()
                MM([(g4(pS[:npv, 0:512], 4)[:, :, :n], kprev[:, kvh, :npv], qnT[:, hs, :n], True, True)],
                   [kpk, "qnT"], [pSk])
                TT("dve", g4(sbS[0][:npv, :], 4)[:, :, :n], g4(pS[:npv, 0:512], 4)[:, :, :n], BT[0][:npv, hs, :n],
                   ALU.add, [pSk, ("BT", 0)], [("sbS", 0)])
                pfree(pSk)
                ACT(g4(PT[0][:npv, :], 4)[:, :, :n], g4(sbS[0][:npv, :], 4)[:, :, :n], AF.Exp, [("sbS", 0)], [("PT", 0)])
                yield
            pS2, pS2k = psum()
            MM([(g4(pS2[:n, 0:512], 4)[:, :, :n], kcur[:, kvh, :n], qnT[:, hs, :n], True, True)], [kck, "qnT"], [pS2k])
            TT("dve", g4(sbS[1][:n, :], 4)[:, :, :n], g4(pS2[:n, 0:512], 4)[:, :, :n], BT[1][:n, hs, :n], ALU.add,
               [pS2k, ("BT", 1)], [("sbS", 1)])
            pfree(pS2k)
            ACT(g4(PT[1][:n, :], 4)[:, :, :n], g4(sbS[1][:n, :], 4)[:, :, :n], AF.Exp, [("sbS", 1)], [("PT", 1)])
            yield
            po, pok = psum()
            pov = po[:, 0:260].rearrange("p (g c) -> p g c", g=4)
            mms = []
            for g in range(4):
                if has_prev:
                    mms.append((pov[:n, g, :], PT[0][:npv, g * 128:g * 128 + n], vprev[:npv, kvh, 0:65], True, False))
                mms.append((pov[:n, g, :], PT[1][:n, g * 128:g * 128 + n], vcur[:n, kvh, 0:65], not has_prev, True))
            MM(mms, [("PT", 0), ("PT", 1), vck, vpk, ("vA1", l, 0), ("vA1", l, 1)], [pok])
            TT("dve", st[:n, 52:56], pov[:n, :, 64:65].rearrange("p g o -> p (g o)"),
               sinkexp[:n, l * 16 + kvh * 4:l * 16 + kvh * 4 + 4], ALU.add, [pok, "sinkexp"], ["st_a3"])
            yield
            RECIP(st[:n, 56:60], st[:n, 52:56], ["st_a3"], ["st_a4"])
            TT("dve", g4(TA2f[:n, kvh * 256:(kvh + 1) * 256], 4), pov[:n, :, 0:64],
               st[:n, 56:60].unsqueeze(2).broadcast_to([n, 4, 64]), ALU.mult, [pok, "st_a4"], [("TA2", kvh)])
            pfree(pok)
            yield
        ACT(TB2b[:n, 0:1024], TA2f[:n, 0:1024], AF.Square, TA2H, ["st_a5", "TB2b"], accum=st[:n, 60:61])
        ACT(st[:n, 62:63], st[:n, 60:61], AF.Ln, ["st_a5"], ["st_a7"], scale=1.0 / 1024, bias=EPS)
        yield
        ACT(st[:n, 63:64], st[:n, 62:63], AF.Exp, ["st_a7"], ["st_a8"], scale=-0.5)
        yield
        ACT(TB2b[:n, :], TA2f[:n, 0:1024], AF.Copy, TA2H + ["st_a8"], ["TB2b"], scale=st[:n, 63:64])
        ph, phk = psum()
        phb = ph[:].bitcast(BF16)
        TR([(phb[:, fc * 128:fc * 128 + n], TB2b[:n, fc * 128:(fc + 1) * 128], identb[:n, :n]) for fc in range(8)],
           ["TB2b", "identb"], [phk])
        TT("dve", actT[:, 8:16, tt * 128:tt * 128 + n], phb[:, 0:1024].rearrange("p (k c) -> p k c", k=8)[:, :, 0:n],
           gao_c[:, l, :].unsqueeze(2).broadcast_to([128, 8, n]), ALU.mult, [phk, "gao_c"], [("aT", tt, 1)])
        pfree(phk)
        yield

    def interleave(gens):
        gens = list(gens)
        while gens:
            for g in list(gens):
                try:
                    next(g)
                except StopIteration:
                    gens.remove(g)

    def run_pass(pinfo):
        kind, si, ti = pinfo
        if kind == "p":
            tiles = [128] * 4
            first = (ti == 0)
            last = (ti == NPASS_SEQ - 1)
        else:
            tiles = [LS] * NS
            first = True
            last = True
        ntt = len(tiles)
        full = all(n == 128 for n in tiles)

        for tt, n in enumerate(tiles):
            src = xp[si, ti * T + tt * 128: ti * T + tt * 128 + n, :] if kind == "p" else xsm[tt, 0:n, :]
            DMA("pool", x[tt][:n, :], src, [], [("x", tt)], "xl%d" % tt)

        for l in range(2):
            P.marks.append(("norm1", pinfo, l, P.nops)); P.phase = "norm1"
            prefetch_extra()
            rmsnorm_all(tiles, l, 0)
            P.marks.append(("inproj", pinfo, l, P.nops)); P.phase = "inproj"
            for j, (c0, knd, cgi) in enumerate(IN_COLS):
                wv, wk_ = next_slab(("win", l, j))
                ncw = 8 if knd == "gts" else 512
                for tt, n in enumerate(tiles):
                    pb, pbk = psum()
                    MM([(pb[:n, 0:ncw], actT[:, kc, tt * 128:tt * 128 + n], wv[:, kc, :], kc == 0, kc == KC - 1)
                        for kc in range(KC)], [wk_, ("aT", tt, 0), ("aT", tt, 1)], [pbk])
                    if knd == "big":
                        TC(alt(), bigp[:n, tt, cgi * 512:(cgi + 1) * 512], pb[:n, :], [pbk], [("big", tt * 8 + cgi)])
                    elif knd == "gts":
                        TC("dve", gts[tt][:n, :], pb[:n, 0:8], [pbk], [("gts", tt)])
                    else:
                        TC(alt(), kvt[tt][:n, :], pb[:n, :], [pbk], [("kvt", tt)])
                    pfree(pbk)
            P.marks.append(("mix", pinfo, l, P.nops)); P.phase = "mix"
            prefetch_extra()

            def stream_m():
                for tt, n in enumerate(tiles):
                    if kind == "s":
                        yield from sample_mlstm_load(l, tt)
                    elif first and tt == 0:
                        prompt_state_init(l)
                    fin = last and (tt == ntt - 1 or kind == "s")
                    if kind == "s" or tt == 0:
                        yield from mlstm_gates(l, tt, n, fin)
                    side = None
                    if kind == "p" and tt + 1 < ntt:
                        fin_n = last and (tt + 1 == ntt - 1)
                        side = mlstm_gates(l, tt + 1, tiles[tt + 1], fin_n)
                    yield from mlstm_tile(l, tt, n, fin, kind, si, side)

            def stream_a():
                for tt, n in enumerate(tiles):
                    if kind == "s":
                        yield from sample_attn_load(l, tt)
                    has_prev = not (kind == "p" and first and tt == 0)
                    par = (ti * 4 + tt) % 2 if kind == "p" else 1
                    fin = last and (tt == ntt - 1 or kind == "s")
                    yield from attn_tile(l, tt, n, has_prev, par, fin, kind, si)

            interleave([stream_m(), stream_a()])
            P.marks.append(("outproj", pinfo, l, P.nops)); P.phase = "outproj"
            for cg in range(NDC):
                wv, wk_ = next_slab(("wout", l, cg))
                for tt, n in enumerate(tiles):
                    pb, pbk = psum()
                    MM([(pb[:n, 0:DCW], actT[:, fc, tt * 128:tt * 128 + n], wv[:, fc, :], fc == 0, fc == 15)
                        for fc in range(16)], [wk_, ("aT", tt, 0), ("aT", tt, 1)], [pbk])
                    xv = x[tt][:n, cg * DCW:(cg + 1) * DCW]
                    TT("dve", xv, pb[:n, 0:DCW], xv, ALU.add, [pbk, ("x", tt)], [("x", tt)])
                    pfree(pbk)
            P.marks.append(("norm2", pinfo, l, P.nops)); P.phase = "norm2"
            prefetch_extra()
            rmsnorm_all(tiles, l, 1)
            P.marks.append(("ffn", pinfo, l, P.nops)); P.phase = "ffn"
            aT_reads = [("aT", tt, h_) for tt in range(ntt) for h_ in range(2)]
            for hf in range(NHALF):
                for j in range(UPS):
                    wv, wk_ = next_slab(("wup", l, hf, j))
                    for h4 in range(4):
                        hc = j * 4 + h4
                        pb, pbk = psum()
                        rk = ("sbS", hc % 2)
                        if full:
                            segs = [(0, ntt * 128)]
                        else:
                            segs = [(tt * 128, tiles[tt]) for tt in range(ntt)]
                        for (c0_, nn_) in segs:
                            MM([(pb[:, c0_:c0_ + nn_], wv[:, kc, h4 * 128:(h4 + 1) * 128], actT[:, kc, c0_:c0_ + nn_],
                                 kc == 0, kc == KC - 1) for kc in range(KC)], [wk_] + aT_reads, [pbk])
                        for (c0_, nn_) in segs:
                            ACT(sbS[hc % 2][:, c0_:c0_ + nn_], pb[:, c0_:c0_ + nn_], AF.Relu, [pbk], [rk])
                        pfree(pbk)
                        for (c0_, nn_) in segs:
                            rv = sbS[hc % 2][:, c0_:c0_ + nn_]
                            TT("pool", uT[:, hc, c0_:c0_ + nn_], rv, rv, ALU.mult, [rk], [("big", hc)])
                for cg in range(NDC):
                    pbs = [psum() for _ in tiles]
                    for s in range(DNS):
                        wv, wk_ = next_slab(("wdn", l, hf, cg, s))
                        for tt, n in enumerate(tiles):
                            pb, pbk = pbs[tt]
                            MM([(pb[:n, 0:DCW], uT[:, s * DNK + hc, tt * 128:tt * 128 + n], wv[:, hc, :],
                                 s == 0 and hc == 0, s == DNS - 1 and hc == DNK - 1) for hc in range(DNK)],
                               [wk_] + [("big", s * DNK + hc) for hc in range(DNK)], [pbk])
                    for tt, n in enumerate(tiles):
                        pb, pbk = pbs[tt]
                        xv = x[tt][:n, cg * DCW:(cg + 1) * DCW]
                        TT("dve", xv, pb[:n, 0:DCW], xv, ALU.add, [pbk, ("x", tt)], [("x", tt)])
                        pfree(pbk)
        for tt, n in enumerate(tiles):
            dst = y_p[si, ti * T + tt * 128: ti * T + tt * 128 + n, :] if kind == "p" else y_s[tt, 0:n, :]
            DMA("pool", dst, x[tt][:n, :], [("x", tt)], [], "so_y%d" % tt)

    P.marks.append(("const_end", P.nops))
    for pinfo in passes:
        run_pass(pinfo)
    assert P.max_ops or wstate["used"] == len(sched)
    P.wait_all("sp", lambda k: k.startswith("so_"))
    print("sbuf bytes remaining", nc.sbuf_bytes_remaining if not callable(nc.sbuf_bytes_remaining) else nc.sbuf_bytes_remaining())
    P.emit()
    es.close()
    return nc, P


_CACHE = {}


def _get_program(cfg):
    key = tuple(sorted(cfg.items()))
    if key not in _CACHE:
        _CACHE[key] = build_program(cfg)
    return _CACHE[key][0]


def run_cores(cfg, ncores, inputs):
    NSEQ = cfg["NSEQ"]; NS = cfg["NS"]
    nc = _get_program(cfg)
    oh = make_onehot()
    f = lambda a: np.ascontiguousarray(np.asarray(a, dtype=np.float32))
    shared = {k: f(inputs[k]) for k in ("rel_bias", "g_mix", "w_in", "b_i", "b_f", "g_q", "g_k", "sinks", "g_mo",
                                        "g_ao", "w_out", "g_ffn", "w_up", "w_down")}
    shared["oh"] = oh
    xp = f(inputs["x_prompt"]); xs = f(inputs["x_sample"])
    ck = f(inputs["cache_k"]); cv = f(inputs["cache_v"])
    sC = f(inputs["state_C"]); sn = f(inputs["state_n"]); sm = f(inputs["state_m"])
    in_maps = []
    for c in range(ncores):
        ps = slice(c * NSEQ, (c + 1) * NSEQ); ss = slice(c * NS, (c + 1) * NS)
        m = dict(shared)
        m["xp"] = np.ascontiguousarray(xp[ps]); m["xs"] = np.ascontiguousarray(xs[ss])
        m["ck"] = np.ascontiguousarray(ck[:, ss].reshape(2, NS, 128, 256))
        m["cv"] = np.ascontiguousarray(cv[:, ss].reshape(2, NS, 128, 256))
        m["sC"] = np.ascontiguousarray(sC[:, ss]); m["sn"] = np.ascontiguousarray(sn[:, ss])
        m["sm"] = np.ascontiguousarray(sm[:, ss])
        in_maps.append(m)
    res = run_bass_kernel_spmd(nc, in_maps, core_ids=list(range(ncores)))
    R = res.results
    cat0 = lambda k: np.concatenate([r[k] for r in R], axis=0)
    cat1 = lambda k: np.concatenate([r[k] for r in R], axis=1)
    nb = ncores * NSEQ; nsb = ncores * NS
    return (cat0("y_p"), cat0("y_s"),
            cat1("p_k").reshape(2, nb, 128, 4, 64), cat1("p_v").reshape(2, nb, 128, 4, 64),
            cat1("p_C"), cat1("p_n"), cat1("p_m"),
            cat1("s_k").reshape(2, nsb, 128, 4, 64), cat1("s_v").reshape(2, nsb, 128, 4, 64),
            cat1("s_C"), cat1("s_n"), cat1("s_m"))


def kernel(**inputs):
    outs = run_cores(full_cfg(), 8, inputs)
    return tuple(np.ascontiguousarray(o, dtype=np.float32) for o in outs)
```
